# Optimizing a Trainium2 kernel written in Bass

```python
import jax, jax.numpy as jnp
from jax import lax
import numpy as np

D_MODEL = 2048
BATCH = 4
SEQ = 2048
DEPTH = 1
DEC_BATCH = 128
DEC_SEQ = 8
PAST_LEN = 16384
PAGE_SIZE = 128

GLA_HEADS = 4
GLA_DK = D_MODEL // 2 // GLA_HEADS
GLA_DV = D_MODEL // GLA_HEADS
GATE_RANK = 16
GATE_TAU = 16.0
GLA_CHUNK = 64
POOL_WIDTH = D_MODEL // 2
POOL_GROUPS = 4
POOL_GROUP_W = POOL_WIDTH // POOL_GROUPS
POOL_OUT_GROUP_W = D_MODEL // POOL_GROUPS
POOL_WINDOWS = (2, 4, 8, 16)
POOL_BUF = max(POOL_WINDOWS) - 1
D_FF = 4 * D_MODEL
EPS = 1e-6

Q_W = GLA_HEADS * GLA_DK
K_W = GLA_HEADS * GLA_DK
V_W = GLA_HEADS * GLA_DV
R_W = GLA_HEADS * GLA_DV
A_W = GATE_RANK
U_W = POOL_WIDTH
GA_W = D_MODEL
GB_W = D_MODEL
IN_W = Q_W + K_W + V_W + R_W + A_W + U_W + GA_W + GB_W
IN_SPLITS = tuple(int(s) for s in np.cumsum([Q_W, K_W, V_W, R_W, A_W, U_W, GA_W]))

kernel_name = "gla_pool_gated_hybrid_step"


def rmsnorm(x, g):
    xf = x.astype(jnp.float32)
    y = xf * lax.rsqrt(jnp.mean(xf * xf, axis=-1, keepdims=True) + EPS)
    return (y * g.astype(jnp.float32)).astype(x.dtype)


def gla_scan(q, k, v, log_a, s0):
    B, T = q.shape[0], q.shape[1]
    c = min(GLA_CHUNK, T)
    n = -(-T // c)
    pad = n * c - T

    def blocks(t):
        t = jnp.pad(t.astype(jnp.float32), ((0, 0), (0, pad), (0, 0), (0, 0)))
        return t.reshape(B, n, c, GLA_HEADS, t.shape[-1]).transpose(1, 0, 3, 2, 4)

    mask = jnp.tril(jnp.ones((c, c), dtype=bool))

    def step(S, inp):
        qc, kc, vc, ac = inp
        b = jnp.cumsum(ac, axis=2)
        diff = b[:, :, :, None, :] - b[:, :, None, :, :]
        decay = jnp.exp(jnp.where(mask[:, :, None], diff, -jnp.inf))
        scores = jnp.einsum('bhtd,bhsd,bhtsd->bhts', qc, kc, decay)
        o = (jnp.einsum('bhts,bhsv->bhtv', scores, vc)
             + jnp.einsum('bhtd,bhdv->bhtv', qc * jnp.exp(b), S))
        b_last = b[:, :, -1:, :]
        S_new = (jnp.exp(b_last[:, :, 0, :])[..., None] * S
                 + jnp.einsum('bhsd,bhsv->bhdv', kc * jnp.exp(b_last - b), vc))
        return S_new, o

    S, o = lax.scan(step, s0.astype(jnp.float32),
                    (blocks(q), blocks(k), blocks(v), blocks(log_a)))
    o = o.transpose(1, 0, 3, 2, 4).reshape(B, n * c, GLA_HEADS, GLA_DV)[:, :T]
    return o, S


def multiscale_pool(u, buf, start_pos):
    T = u.shape[1]
    ext = jnp.concatenate([buf.astype(jnp.float32), u.astype(jnp.float32)], axis=1)
    csum = jnp.pad(jnp.cumsum(ext, axis=1), ((0, 0), (1, 0), (0, 0)))
    pos = start_pos + jnp.arange(T) + 1
    outs = []
    for g, w in enumerate(POOL_WINDOWS):
        lo, hi = g * POOL_GROUP_W, (g + 1) * POOL_GROUP_W
        s = (csum[:, POOL_BUF + 1:POOL_BUF + 1 + T, lo:hi]
             - csum[:, POOL_BUF + 1 - w:POOL_BUF + 1 - w + T, lo:hi])
        cnt = jnp.minimum(pos, w).astype(jnp.float32)[None, :, None]
        outs.append(s / cnt)
    pooled = jnp.concatenate(outs, axis=-1)
    return (pooled - u.astype(jnp.float32)).astype(u.dtype)


def layer(x, s_gla0, pool_buf, start_pos, norm_mix_g, w_in, w_alpha_up, b_alpha,
          gla_norm_g, pool_w, pool_scale, w_out, norm_mlp_g, w_up, w_down, norm_final_g):
    B, T, _ = x.shape
    h = rmsnorm(x, norm_mix_g)
    z = h @ w_in
    q, k, v, r, a_lr, u, ga, gb = jnp.split(z, IN_SPLITS, axis=-1)
    q = q.reshape(B, T, GLA_HEADS, GLA_DK) * (GLA_DK ** -0.5)
    k = k.reshape(B, T, GLA_HEADS, GLA_DK)
    v = v.reshape(B, T, GLA_HEADS, GLA_DV)
    a_logit = (a_lr @ w_alpha_up + b_alpha).astype(jnp.float32)
    log_a = (jax.nn.log_sigmoid(a_logit) / GATE_TAU).reshape(B, T, GLA_HEADS, GLA_DK)
    o, s_gla = gla_scan(q, k, v, log_a, s0=s_gla0)
    o = rmsnorm(o, gla_norm_g).reshape(B, T, V_W).astype(x.dtype)
    o_a = o * jax.nn.silu(r)
    p = multiscale_pool(u, pool_buf, start_pos)
    o_b = jnp.einsum('btgc,gcd->btgd', p.reshape(B, T, POOL_GROUPS, POOL_GROUP_W),
                     pool_w).reshape(B, T, D_MODEL) * pool_scale
    m = jax.nn.sigmoid(ga) * o_a + jax.nn.sigmoid(gb) * o_b
    x = x + m @ w_out
    h2 = rmsnorm(x, norm_mlp_g)
    x = x + jnp.square(jax.nn.relu(h2 @ w_up)) @ w_down
    y = rmsnorm(x, norm_final_g)
    new_buf = jnp.concatenate([pool_buf.astype(u.dtype), u], axis=1)[:, -POOL_BUF:]
    return y, s_gla, new_buf


def setup_inputs(seed: int = 0) -> dict:
    key = jax.random.key(seed)
    ks = jax.random.split(key, 16)
    f32 = jnp.float32
    nrm = lambda k, shape, scale: jax.random.normal(k, shape, f32) * scale
    return {
        "x_prompt": nrm(ks[0], (BATCH, SEQ, D_MODEL), 1.0),
        "x_sample": nrm(ks[1], (DEC_BATCH, DEC_SEQ, D_MODEL), 1.0),
        "state_gla": nrm(ks[2], (DEC_BATCH, GLA_HEADS, GLA_DK, GLA_DV), 1.0),
        "state_pool": nrm(ks[3], (DEC_BATCH, POOL_BUF, POOL_WIDTH), 1.0),
        "norm_mix_g": 1.0 + nrm(ks[4], (D_MODEL,), 0.02),
        "w_in": nrm(ks[5], (D_MODEL, IN_W), D_MODEL ** -0.5),
        "w_alpha_up": nrm(ks[6], (GATE_RANK, Q_W), GATE_RANK ** -0.5),
        "b_alpha": nrm(ks[7], (Q_W,), 0.1),
        "gla_norm_g": 1.0 + nrm(ks[8], (GLA_DV,), 0.02),
        "pool_w": nrm(ks[9], (POOL_GROUPS, POOL_GROUP_W, POOL_OUT_GROUP_W), POOL_GROUP_W ** -0.5),
        "pool_scale": 1.0 + nrm(ks[10], (D_MODEL,), 0.1),
        "w_out": nrm(ks[11], (D_MODEL, D_MODEL), D_MODEL ** -0.5),
        "norm_mlp_g": 1.0 + nrm(ks[12], (D_MODEL,), 0.02),
        "w_up": nrm(ks[13], (D_MODEL, D_FF), D_MODEL ** -0.5),
        "w_down": nrm(ks[14], (D_FF, D_MODEL), D_FF ** -0.5),
        "norm_final_g": 1.0 + nrm(ks[15], (D_MODEL,), 0.02),
    }


def reference(x_prompt, x_sample, state_gla, state_pool, norm_mix_g, w_in, w_alpha_up,
              b_alpha, gla_norm_g, pool_w, pool_scale, w_out, norm_mlp_g, w_up, w_down,
              norm_final_g):
    params = (norm_mix_g, w_in, w_alpha_up, b_alpha, gla_norm_g, pool_w, pool_scale,
              w_out, norm_mlp_g, w_up, w_down, norm_final_g)
    y_p, y_s = x_prompt, x_sample
    s_gla_p = jnp.zeros((BATCH, GLA_HEADS, GLA_DK, GLA_DV), jnp.float32)
    buf_p = jnp.zeros((BATCH, POOL_BUF, POOL_WIDTH), x_prompt.dtype)
    s_gla_s, buf_s = state_gla, state_pool
    for _ in range(DEPTH):
        y_p, s_gla_p, buf_p = layer(y_p, s_gla_p, buf_p, 0, *params)
        y_s, s_gla_s, buf_s = layer(y_s, s_gla_s, buf_s, PAST_LEN, *params)
    return (y_p, y_s, s_gla_p, buf_p, s_gla_s, buf_s)
```

```python
import contextlib
import numpy as np
import concourse.bass as bass
import concourse.mybir as mybir
from concourse.bass_utils import run_bass_kernel_spmd

F32 = mybir.dt.float32
BF16 = mybir.dt.bfloat16
AF = mybir.ActivationFunctionType
ALU = mybir.AluOpType

P = 128
D = 2048
KC = D // P
H = 4
DK = 256
DV = 512
IN_W = 11280
EPS = 1e-6
LN_QSCALE = float(np.log(DK ** -0.5))
WB = 256
HCOLS = 2816
ZQ, ZK, ZV, ZR, ZGA, ZGB, ZU = 0, 256, 512, 1024, 1536, 2048, 2560

C_ID, C_MASKP, C_CUMP, C_MASKS, C_CUMS = 0, 1, 2, 3, 4
C_BPREV, C_BCUR, C_BFIRST, C_BSC, C_BSB = 5, 9, 13, 17, 21
NCONST = 29


class Cfg:
    def __init__(self, npre=8, nmain=8, ff=8192, fsb=1024, sbs=None, ntmax=None):
        self.npre = npre
        self.nmain = nmain
        self.nt = nmain + 1
        self.ntok = self.nt * P
        self.ff = ff
        self.fsb = fsb
        self.sbs = sbs or [[0, 1, 2, 3, 4], [5, 6, 7, 8]]
        self.ntmax = max(len(s) for s in self.sbs)
        self.ntmax = max(self.ntmax, 1, ntmax or 1)


class Prog:
    COMPUTE = ("pe", "act", "dve", "pool")

    def __init__(self, nc, es):
        self.nc = nc
        self.es = es
        self.eng = {"pe": nc.tensor, "act": nc.scalar, "dve": nc.vector, "pool": nc.gpsimd, "sp": nc.sync}
        self.ops = []
        self.last_w = {}
        self.readers = {}
        self.dma_cnt = {}
        self.sems = {}
        self.fence_ops = {}
        self.snaps = {}
        self.cur_after = None

    def _sem(self, key):
        if key not in self.sems:
            self.sems[key] = self.es.enter_context(self.nc.semaphore("s_" + key))
        return self.sems[key]

    def _add(self, rec, reads, writes):
        oid = len(self.ops)
        deps = set()
        for t in reads:
            w = self.last_w.get(t)
            if w is not None:
                deps.add(w)
        for t in writes:
            w = self.last_w.get(t)
            if w is not None:
                deps.add(w)
            for r in self.readers.get(t, ()):
                deps.add(r)
        f = self.fence_ops.pop(rec["eng"], None)
        if f:
            deps.update(f)
        if self.cur_after is not None:
            deps.update(self.snaps[self.cur_after])
        rec["deps"] = deps
        rec["signal"] = False
        self.ops.append(rec)
        for t in reads:
            self.readers.setdefault(t, []).append(oid)
        for t in writes:
            self.last_w[t] = oid
            self.readers[t] = []
        return oid

    def op(self, eng, fn, reads=(), writes=()):
        return self._add({"eng": eng, "fn": fn, "kind": "c"}, list(reads), list(writes))

    def dma(self, q, out, in_, key, reads=(), writes=(), nc_ok=False):
        self.dma_cnt[key] = self.dma_cnt.get(key, 0) + 16
        rec = {"eng": q, "kind": "d", "out": out, "in": in_, "key": key, "ticket": self.dma_cnt[key], "nc_ok": nc_ok}
        return self._add(rec, list(reads), list(writes))

    def fence(self):
        last = {}
        for i, o in enumerate(self.ops):
            last[o["eng"]] = i
        dmas = [i for i, o in enumerate(self.ops) if o["kind"] == "d" and not o.get("fenced")]
        for i in dmas:
            self.ops[i]["fenced"] = True
        allp = set(last.values()) | set(dmas)
        for e in self.eng:
            prev = self.fence_ops.get(e, set())
            self.fence_ops[e] = set(prev) | allp

    def snapshot(self, tag):
        last = {}
        for i, o in enumerate(self.ops):
            last[o["eng"]] = i
        dmas = [i for i, o in enumerate(self.ops) if o["kind"] == "d" and not o.get("fenced")]
        self.snaps[tag] = set(last.values()) | set(dmas)

    def emit(self):
        ops = self.ops
        for o in ops:
            for d in o["deps"]:
                od = ops[d]
                if od["kind"] == "c":
                    if od["eng"] == o["eng"] and o["eng"] == "pe":
                        continue
                    od["signal"] = True
        cnt = {e: 0 for e in self.COMPUTE}
        for o in ops:
            if o["kind"] == "c" and o["signal"]:
                cnt[o["eng"]] += 1
                o["ticket"] = cnt[o["eng"]]
        waited = {e: {} for e in self.eng}
        nwaits = 0
        for o in ops:
            e = o["eng"]
            eobj = self.eng[e]
            need = {}
            for d in o["deps"]:
                od = ops[d]
                if od["kind"] == "c":
                    if od["eng"] == e and e == "pe":
                        continue
                    k = "E" + od["eng"]
                else:
                    k = "D" + od["key"]
                need[k] = max(need.get(k, 0), od["ticket"])
            for k, v in need.items():
                if waited[e].get(k, 0) >= v:
                    continue
                eobj.wait_ge(self._sem(k), v)
                waited[e][k] = v
                nwaits += 1
            if o["kind"] == "c":
                inst = o["fn"](eobj)
                if o["signal"]:
                    inst.then_inc(self._sem("E" + e), 1)
            else:
                if o["nc_ok"]:
                    with self.nc.allow_non_contiguous_dma(reason="tiny one-time parameter layout"):
                        inst = eobj.dma_start(out=o["out"], in_=o["in"])
                else:
                    inst = eobj.dma_start(out=o["out"], in_=o["in"])
                inst.then_inc(self._sem("D" + o["key"]), 16)
        sp = self.eng["sp"]
        for key, v in self.dma_cnt.items():
            if waited["sp"].get("D" + key, 0) < v:
                sp.wait_ge(self._sem("D" + key), v)
        for e in self.COMPUTE:
            if cnt[e] and waited["sp"].get("E" + e, 0) < cnt[e]:
                sp.wait_ge(self._sem("E" + e), cnt[e])
        return len(ops), nwaits


class Arena:
    def __init__(self, tensor, nbytes):
        self.t = tensor
        self.n = nbytes
        self.top = 0
        self.peak = 0

    def alloc(self, cols, dt, parts=P):
        esz = 4 if dt == F32 else 2
        nb = (cols * esz + 31) // 32 * 32
        off = self.top
        self.top += nb
        self.peak = max(self.peak, self.top)
        assert self.top <= self.n, f"SBUF arena overflow: {self.top} > {self.n}"
        v = self.t[0:parts, off // 4:(off + nb) // 4]
        if dt != F32:
            v = v.bitcast(dt)
        return v[:, 0:cols]


def build_program(cfg):
    nc = bass.Bass("TRN2", target_bir_lowering=False)
    NT, NTOK, NPRE, NMAIN = cfg.nt, cfg.ntok, cfg.npre, cfg.nmain
    FF, FSB = cfg.ff, cfg.fsb
    NS = 16

    def din(name, shape):
        return nc.dram_tensor(name, list(shape), F32, kind="ExternalInput").ap()

    def dout(name, shape):
        return nc.dram_tensor(name, list(shape), F32, kind="ExternalOutput").ap()

    xpre = din("xpre", [NPRE * P, D])
    xmain = din("xmain", [NTOK, D])
    sgla = din("sgla", [NS, H, DK, DV])
    spool = din("spool", [NS, 15, 1024])
    norm_mix_g = din("norm_mix_g", [D])
    NBLK_IN = 11 * H
    w_in_blk = din("w_in_blk", [NBLK_IN, P, KC, WB])
    w_alr = din("w_alr", [P, KC, 16])
    w_alpha = din("w_alpha_up", [16, 1024])
    b_alpha = din("b_alpha", [1024])
    gla_g = din("gla_norm_g", [DV])
    pool_w = din("pool_w", [4, 256, 512])
    pool_scale = din("pool_scale", [D])
    w_out_blk = din("w_out_blk", [D // WB, P, KC, WB])
    norm_mlp_g = din("norm_mlp_g", [D])
    w_up_blk = din("w_up_blk", [FF // WB, P, KC, WB])
    w_down_blk = din("w_down_blk", [(FF // FSB) * (D // WB), P, FSB // P, WB])
    norm_final_g = din("norm_final_g", [D])
    cmat = din("cmat", [NCONST, P, P])
    selr_d = din("selr", [P, 16])

    y = dout("y", [NTOK, D])
    o_sgp = dout("o_sgp", [H, DK, DV])
    o_plp = dout("o_plp", [15, 1024])
    o_sgs = dout("o_sgs", [NS, H, DK, DV])
    o_pls = dout("o_pls", [NS, 15, 1024])

    es = contextlib.ExitStack()
    with es:
        ARENA_BYTES = 207 * 1024
        arena_t = es.enter_context(nc.sbuf_tensor("arena", [P, ARENA_BYTES // 4], F32))
        A = Arena(arena_t, ARENA_BYTES)
        ps = [es.enter_context(nc.psum_tensor(f"ps{i}", [P, 512], F32))[:, :] for i in range(8)]
        pg = Prog(nc, es)

        def psb(i, lo, n):
            return ps[i][:, lo:lo + n // 2].bitcast(BF16)

        cm = A.alloc(NCONST * P, BF16).rearrange("p (n c) -> p n c", n=NCONST)
        identF = A.alloc(P, F32)
        onesF = A.alloc(P, F32)
        ones_b = A.alloc(8, BF16)
        walpha = A.alloc(1024, BF16)
        glag = A.alloc(DV, F32)
        selr = A.alloc(16, F32)

        pg.dma("pool", cm, cmat.rearrange("n p c -> p n c"), "cm", writes=["cm"])
        pg.dma("sp", identF, cmat[C_ID], "idf", writes=["identF"])
        pg.dma("pool", walpha[0:16, :], w_alpha, "wa", writes=["walpha"])
        pg.dma("pool", walpha[16:17, :], b_alpha.rearrange("(o n) -> o n", o=1), "wa", writes=["walpha"])
        pg.dma("sp", glag, gla_g.partition_broadcast(P), "glag", writes=["glag"])
        pg.dma("sp", selr, selr_d, "selr", writes=["selr"])
        pg.op("dve", lambda e: e.memset(onesF, 1.0), writes=["onesF"])
        pg.op("dve", lambda e: e.memset(ones_b, 1.0), writes=["ones_b"])

        def CM(i):
            return cm[:, i, :]

        MT_COLS = max(KC * NTOK, 4 * D, (FSB // P) * NTOK)
        mT_flat = A.alloc(MT_COLS, BF16)
        mT = mT_flat[:, 0:KC * NTOK].rearrange("p (k t) -> p k t", k=KC)
        mark_B = A.top

        def rstd_chain(ss, tmp, rstd, n, tag, nparts=P):
            pg.op("dve", lambda e: e.tensor_scalar(out=tmp, in0=ss, scalar1=1.0 / n, scalar2=EPS,
                                                  op0=ALU.mult, op1=ALU.add), reads=[tag + "ss"], writes=[tag + "tmp"])
            pg.op("act", lambda e: e.activation(out=tmp, in_=tmp, func=AF.Ln), reads=[tag + "tmp"], writes=[tag + "tmp"])
            pg.op("act", lambda e: e.activation(out=rstd, in_=tmp, func=AF.Exp, scale=-0.5),
                  reads=[tag + "tmp"], writes=[tag + "rstd"])

        def phase_A_chunks(xsrc, tiles, slot_of, gb_mix, xt, xn, pfx, after=None):
            gtok = pfx + "gbmix"

            def s0():
                pg.dma("sp", gb_mix, norm_mix_g.partition_broadcast(P), pfx + "gbm", writes=[gtok])

            def s1(n, ti):
                b = n % 2
                bx = n % len(xn)
                xtb, xnb = xt[b], xn[bx]
                pg.dma("sp", xtb, xsrc[ti * P:(ti + 1) * P, :], f"{pfx}xt{b}", writes=[f"{pfx}xt{b}"])
                ss, tmp, rstd = stat[:, 8 * b:8 * b + 1], stat[:, 8 * b + 1:8 * b + 2], stat[:, 8 * b + 2:8 * b + 3]
                tg = f"A{b}"
                if n < 2:
                    pg.op("dve", lambda e: e.memset(ss, 0.0), writes=[tg + "ss"])
                pg.op("act", lambda e: e.activation(out=xnb, in_=xtb, func=AF.Square, accum_out=ss),
                      reads=[f"{pfx}xt{b}"], writes=[tg + "ss", f"{pfx}xn{bx}"])
                rstd_chain(ss, tmp, rstd, D, tg)
                pg.op("dve", lambda e: e.memset(ss, 0.0), reads=[tg + "tmp"], writes=[tg + "ss"])
                pg.op("dve", lambda e: e.scalar_tensor_tensor(out=xnb, in0=xtb, scalar=rstd, in1=gb_mix, op0=ALU.mult, op1=ALU.mult),
                      reads=[f"{pfx}xt{b}", tg + "rstd", gtok], writes=[f"{pfx}xn{bx}"])

            def s2(n, ti):
                bx = n % len(xn)
                xnb = xn[bx]
                sl = slot_of(ti)
                for g4 in range(4):
                    pb = g4 % 2
                    pt = psb(pb, 0, 512).rearrange("p (k c) -> p k c", k=4)

                    def tr(e, g4=g4, pt=pt):
                        inst = None
                        for j in range(4):
                            kc = g4 * 4 + j
                            inst = e.transpose(pt[:, j, :], xnb[:, kc * P:(kc + 1) * P], CM(C_ID))
                        return inst
                    pg.op("pe", tr, reads=[f"{pfx}xn{bx}", "cm"], writes=[f"ps{pb}"])
                    dst = hTs(sl)[:, g4 * 4:(g4 + 1) * 4, :]
                    if g4 % 2 == 0:
                        pg.op("act", lambda e, dst=dst, pt=pt: e.copy(out=dst, in_=pt), reads=[f"ps{pb}"], writes=[f"hT{sl}"])
                    else:
                        pg.op("dve", lambda e, dst=dst, pt=pt: e.tensor_copy(out=dst, in_=pt), reads=[f"ps{pb}"], writes=[f"hT{sl}"])

            tl = list(tiles)

            def wrap(fn):
                def g():
                    prev = pg.cur_after
                    pg.cur_after = after
                    try:
                        fn()
                    finally:
                        pg.cur_after = prev
                return g
            out = []
            if tl:
                out.append(wrap(lambda: (s0(), s1(0, tl[0]))))
            for n, ti in enumerate(tl):
                def ch(n=n, ti=ti):
                    if len(xn) > 1 and n + 1 < len(tl):
                        s1(n + 1, tl[n + 1])
                    s2(n, ti)
                    if len(xn) == 1 and n + 1 < len(tl):
                        s1(n + 1, tl[n + 1])
                out.append(wrap(ch))
            return out

        def phase_A(xsrc, tiles, slot_of):
            for ch in phase_A_chunks(xsrc, tiles, slot_of, gbmix, xt, xn, "P"):
                ch()

        wctr = [0]

        def load_w(dst_view, src_ap):
            s = wctr[0] % 2
            wctr[0] += 1
            pg.dma("pool", dst_view(s), src_ap, f"w{s}", writes=[f"w{s}"])
            return s

        zctr = [0]

        def dense_block(hT, slots, wsl, wview, ncols, dst_fn, ztoks, extra_fp32=None):
            for n, sl in enumerate(slots):
                pb = zctr[0] % 2
                zctr[0] += 1
                pz = ps[pb][:, 0:ncols]

                def mm(e, sl=sl, pz=pz):
                    inst = None
                    for kc in range(KC):
                        inst = e.matmul(pz, lhsT=hTs(sl)[:, kc, :], rhs=wview[:, kc, 0:ncols],
                                        start=(kc == 0), stop=(kc == KC - 1))
                    return inst
                pg.op("pe", mm, reads=[f"hT{sl}", f"w{wsl}"], writes=[f"ps{pb}"])
                dst = dst_fn(n)
                ex = extra_fp32(n) if extra_fp32 is not None else None
                use_act = ((zctr[0] % 2) == 0) or (ex is not None)
                if dst is not None:
                    if use_act:
                        pg.op("act", lambda e, dst=dst, pz=pz: e.copy(out=dst, in_=pz), reads=[f"ps{pb}"], writes=[ztoks[n]])
                    else:
                        pg.op("dve", lambda e, dst=dst, pz=pz: e.tensor_copy(out=dst, in_=pz), reads=[f"ps{pb}"], writes=[ztoks[n]])
                if ex is not None:
                    exdst, extok = ex
                    pg.op("act", lambda e, exdst=exdst, pz=pz: e.copy(out=exdst, in_=pz), reads=[f"ps{pb}"], writes=[extok])

        NSLOT_H = cfg.ntmax + 1
        hT = A.alloc(KC * NSLOT_H * P, BF16).rearrange("p (k t) -> p k t", k=KC)
        zbuf = [A.alloc(cfg.ntmax * HCOLS, BF16).rearrange("p (n c) -> p n c", n=cfg.ntmax) for _ in range(2)]
        wbuf = [A.alloc(KC * WB, BF16).rearrange("p (k c) -> p k c", k=KC) for _ in range(2)]
        _zb1 = zbuf[1].rearrange("p n c -> p (n c)")
        _zcols = cfg.ntmax * HCOLS
        N_XS = max(0, min(NPRE - NSLOT_H, (_zcols - NPRE * 768) // (KC * P)))
        hTx = [_zb1[:, _zcols - (i + 1) * KC * P:_zcols - i * KC * P].rearrange("p (k t) -> p k t", k=KC) for i in range(N_XS)]

        def hTs(sl):
            if sl < NSLOT_H:
                return hT[:, :, sl * P:(sl + 1) * P]
            return hTx[sl - NSLOT_H]
        S32 = A.alloc(2 * DV, F32).rearrange("p (c v) -> p c v", c=2)
        t_Sx = [A.alloc(DV, F32) for _ in range(4)]
        Sbf = A.alloc(2 * DV, BF16).rearrange("p (c v) -> p c v", c=2)
        ucarry = A.alloc(H * 256, BF16).rearrange("p (h c) -> p h c", h=H)
        alrT = A.alloc(max(NSLOT_H, NPRE) * P, BF16)
        poolw = A.alloc(2 * DV, BF16).rearrange("p (c v) -> p c v", c=2)
        pscale = A.alloc(DV, F32)
        bufS = A.alloc(2 * 1024, BF16).rearrange("p (f c) -> p f c", f=2)
        stat = A.alloc(16, F32)
        mark_tmp = A.top
        gbmix = A.alloc(D, F32)
        xt = [A.alloc(D, F32) for _ in range(2)]
        xn = [A.alloc(D, BF16) for _ in range(2)]
        A.top = mark_tmp
        t_eA = A.alloc(256, F32)
        t_sp = A.alloc(256, BF16)
        t_Ek = t_eA
        t_Eq = A.alloc(256, F32)
        t_ebl = A.alloc(32, F32)
        t_ke = A.alloc(256, BF16)
        t_qe = A.alloc(256, BF16)
        t_qkT = A.alloc(512, BF16).rearrange("p (k c) -> p k c", k=4)
        t_scT = A.alloc(P, BF16)
        t_f1 = A.alloc(DV, F32)
        t_f2 = A.alloc(DV, F32)
        t_f3 = A.alloc(DV, F32)
        t_m = A.alloc(DV, BF16)
        t_pT = A.alloc(256, BF16).rearrange("p (k c) -> p k c", k=2)
        t_tS = A.alloc(2 * DV, F32).rearrange("p (c v) -> p c v", c=2)
        t_junk = A.alloc(DV, BF16)
        t_u32 = A.alloc(256, F32)
        t_u32s = t_u32
        t_ma = A.alloc(DV, F32)
        t_Sj = A.alloc(2 * DV, F32).rearrange("p (c v) -> p c v", c=2)
        t_Sjb = A.alloc(8 * DV, BF16).rearrange("p (c v) -> p c v", c=8)
        t_qm = A.alloc(256, BF16).rearrange("p (k c) -> p k c", k=2)
        t_km = A.alloc(256, BF16)
        assert A.top >= mark_tmp + (3 * D * 4 + 2 * D * 2) or True

        _zb0 = zbuf[0].rearrange("p n c -> p (n c)")
        CAN_OVL = (cfg.ntmax * HCOLS >= 6 * D) and (A.n - A.top >= 2 * D + 64)
        if CAN_OVL:
            xtB = [_zb0[:, 0:2 * D].bitcast(F32), _zb0[:, 2 * D:4 * D].bitcast(F32)]
            gbmixB = _zb0[:, 4 * D:6 * D].bitcast(F32)
            xnB = A.alloc(D, BF16)
        pg.op("dve", lambda e: e.memset(alrT[0:32, :], 1.0), writes=["alrT"])
        pg.dma("pool", bufS[0:120, 0, :], spool[0:8].rearrange("s r c -> (s r) c"), "bufS", writes=["bufS"])
        pg.dma("pool", bufS[0:120, 1, :], spool[8:16].rearrange("s r c -> (s r) c"), "bufS", writes=["bufS"])
        pg.dma("sp", o_pls[:, 0:7, :], spool[:, 8:15, :], "opl0")

        def alr_block(slots):
            s = load_w(lambda s: wbuf[s][:, :, 0:16], w_alr)
            wv = wbuf[s]
            for n, sl in enumerate(slots):
                pa = ps[2][0:16, 0:P]

                def mm(e, sl=sl, pa=pa):
                    inst = None
                    for kc in range(KC):
                        inst = e.matmul(pa, lhsT=wv[:, kc, 0:16], rhs=hTs(sl)[:, kc, :],
                                        start=(kc == 0), stop=(kc == KC - 1))
                    return inst
                pg.op("pe", mm, reads=[f"hT{sl}", f"w{s}"], writes=["ps2"])
                dst = alrT[0:16, n * P:(n + 1) * P]
                pg.op("act", lambda e, dst=dst, pa=pa: e.copy(out=dst, in_=pa), reads=["ps2"], writes=["alrT"])

        def gla_step(h, V, ztok, n, kind, full, uprev=None, uprev_tok=None, band_first=False, tile_glob=None, part="all"):
            cum = CM(C_CUMP if kind == "P" else C_CUMS)
            mask = CM(C_MASKP if kind == "P" else C_MASKS)
            NL = 1 if kind == "P" else NS
            cumlast = cum[:, P - 1:P] if kind == "P" else cum.rearrange("p (j i) -> p j i", i=8)[:, :, 7]
            k, v = V["k"], V["v"]
            po = ps[4]
            pP = [ps[5], ps[6]]
            if full:
                r, ga, gb, u = V["r"], V["ga"], V["gb"], V["u"]
                q = V["q"]
            if part != "back":
                pa = ps[2][:, 0:256]
                pg.op("pe", lambda e: e.matmul(pa, lhsT=alrT[0:17, n * P:(n + 1) * P], rhs=walpha[0:17, h * 256:(h + 1) * 256],
                                               start=True, stop=True), reads=["alrT", "walpha"], writes=["ps2"])
                pg.op("act", lambda e: e.activation(out=t_eA, in_=pa, func=AF.Exp, scale=-1.0), reads=["ps2"], writes=["eA", "Ek"])
                pg.op("act", lambda e: e.activation(out=t_sp, in_=t_eA, func=AF.Ln, bias=1.0), reads=["eA"], writes=["sp"])
                if full:
                    r, ga, gb, u = V["r"], V["ga"], V["gb"], V["u"]
                    pg.op("act", lambda e: e.activation(out=t_f1, in_=r, func=AF.Exp, scale=-1.0), reads=[ztok], writes=["f1"])
                    pg.op("act", lambda e: e.activation(out=t_f2, in_=ga, func=AF.Exp, scale=-1.0), reads=[ztok], writes=["f2"])
                    pg.op("act", lambda e: e.activation(out=t_f1, in_=t_f1, func=AF.Ln, bias=1.0), reads=["f1"], writes=["f1"])
                    pg.op("act", lambda e: e.activation(out=t_f2, in_=t_f2, func=AF.Ln, bias=1.0), reads=["f2"], writes=["f2"])
                yield
                pb_ = ps[2][:, 256:512]
                pg.op("pe", lambda e: e.matmul(pb_, lhsT=cum, rhs=t_sp, start=True, stop=True), reads=["sp", "cm"], writes=["ps2"])
                pe_ = ps[3][:, 384:384 + 2 * NL]

                def mm_bl(e):
                    inst = None
                    for dc in range(2):
                        inst = e.matmul(pe_[:, dc * NL:(dc + 1) * NL], lhsT=t_sp[:, dc * P:(dc + 1) * P], rhs=cumlast,
                                        start=True, stop=True)
                    return inst
                pg.op("pe", mm_bl, reads=["sp", "cm"], writes=["ps3"])
                pg.op("act", lambda e: e.activation(out=t_Ek, in_=pb_, func=AF.Exp, scale=-1.0), reads=["ps2", "sp"], writes=["Ek", "eA"])
                if full:
                    pg.op("act", lambda e: e.activation(out=t_Eq, in_=pb_, func=AF.Exp, bias=LN_QSCALE), reads=["ps2"], writes=["Eq"])
                pg.op("act", lambda e: e.activation(out=t_ebl[:, 0:2 * NL], in_=pe_, func=AF.Exp), reads=["ps3"], writes=["ebl"])
                pg.op("dve", lambda e: e.tensor_tensor(out=t_ke, in0=k, in1=t_Ek, op=ALU.mult), reads=[ztok, "Ek"], writes=["ke"])
                if full:
                    q = V["q"]
                    pg.op("dve", lambda e: e.tensor_tensor(out=t_qe, in0=q, in1=t_Eq, op=ALU.mult), reads=[ztok, "Eq"], writes=["qe"])
                    pg.op("dve", lambda e: e.tensor_tensor(out=t_f1, in0=t_f1, in1=t_f2, op=ALU.add), reads=["f1", "f2"], writes=["f1"])
                    pg.op("act", lambda e: e.activation(out=t_f1, in_=t_f1, func=AF.Exp, scale=-1.0), reads=["f1"], writes=["f1"])
                    pg.op("dve", lambda e: e.tensor_tensor(out=t_f2, in0=r, in1=t_f1, op=ALU.mult), reads=[ztok, "f1"], writes=["f2"])
                    pg.op("dve", lambda e: e.tensor_tensor(out=t_f2, in0=t_f2, in1=glag, op=ALU.mult), reads=["f2", "glag"], writes=["f2"])
                yield
                if full:
                    ptq = psb(3, 0, 512).rearrange("p (k c) -> p k c", k=4)

                    def trq(e):
                        e.transpose(ptq[:, 0, :], t_qe[:, 0:P], CM(C_ID))
                        e.transpose(ptq[:, 1, :], t_qe[:, P:2 * P], CM(C_ID))
                        e.transpose(ptq[:, 2, :], t_ke[:, 0:P], CM(C_ID))
                        return e.transpose(ptq[:, 3, :], t_ke[:, P:2 * P], CM(C_ID))
                    pg.op("pe", trq, reads=["qe", "ke", "cm"], writes=["ps3"])
                    pg.op("act", lambda e: e.copy(out=t_qkT, in_=ptq), reads=["ps3"], writes=["qkT"])
                    yield
                    pss = ps[3][:, 256:384]

                    def mms(e):
                        e.matmul(pss, lhsT=t_qkT[:, 2, :], rhs=t_qkT[:, 0, :], start=True, stop=False)
                        return e.matmul(pss, lhsT=t_qkT[:, 3, :], rhs=t_qkT[:, 1, :], start=False, stop=True)
                    pg.op("pe", mms, reads=["qkT"], writes=["ps3"])
                    pg.op("dve", lambda e: e.tensor_tensor(out=t_scT, in0=pss, in1=mask, op=ALU.mult), reads=["ps3", "cm"], writes=["scT"])
                    yield
            if part == "front":
                return
            if kind == "P":
                if full:
                    def mmo(e):
                        e.matmul(po, lhsT=t_scT, rhs=v, start=True, stop=False)
                        e.matmul(po, lhsT=t_qkT[:, 0, :], rhs=Sbf[:, 0, :], start=False, stop=False)
                        return e.matmul(po, lhsT=t_qkT[:, 1, :], rhs=Sbf[:, 1, :], start=False, stop=True)
                    pg.op("pe", mmo, reads=["scT", "qkT", ztok, "Sbf"], writes=["ps4"])
                def mmp(e):
                    e.matmul(pP[0], lhsT=t_ke[:, 0:P], rhs=v, start=True, stop=True)
                    return e.matmul(pP[1], lhsT=t_ke[:, P:2 * P], rhs=v, start=True, stop=True)
                pg.op("pe", mmp, reads=["ke", ztok], writes=["ps5", "ps6"])
                for dc in range(2):
                    pg.op("dve", lambda e, dc=dc: e.tensor_tensor(out=t_tS[:, dc, :], in0=S32[:, dc, :], in1=pP[dc], op=ALU.add),
                          reads=["S32", f"ps{5 + dc}"], writes=[f"tS{dc}"])
                    pg.op("dve", lambda e, dc=dc: e.tensor_scalar(out=S32[:, dc, :], in0=t_tS[:, dc, :], scalar1=t_ebl[:, dc:dc + 1],
                                                                   scalar2=None, op0=ALU.mult),
                          reads=[f"tS{dc}", "ebl"], writes=["S32"])
                    pg.op("act", lambda e, dc=dc: e.mul(out=Sbf[:, dc, :], in_=t_tS[:, dc, :], mul=t_ebl[:, dc:dc + 1]),
                          reads=[f"tS{dc}", "ebl"], writes=["Sbf"])
            else:
                pg.op("pe", lambda e: e.matmul(po, lhsT=t_scT, rhs=v, start=True, stop=False),
                      reads=["scT", ztok], writes=["ps4"])
                items = [(j, dc) for j in range(NS) for dc in range(2)]
                NB = 8
                fbuf = [t_Sj[:, 0, :], t_Sj[:, 1, :], t_tS[:, 0, :], t_tS[:, 1, :]] + t_Sx
                ftok = ["Sj0", "Sj1", "tS0", "tS1", "Sx0", "Sx1", "Sx2", "Sx3"]
                qmb = [t_qm, t_junk[:, 0:256].rearrange("p (k c) -> p k c", k=2)]
                kmb = [t_km, t_junk[:, 256:512]]

                def s_loads(i):
                    j, dc = items[i]
                    bb = i % NB
                    pg.dma("sp", fbuf[bb], sgla[j, h, dc * P:(dc + 1) * P, :], f"SL{bb}", writes=[ftok[bb]])

                def s_cast(i):
                    bb = i % NB
                    pg.op("act", lambda e: e.copy(out=t_Sjb[:, bb, :], in_=fbuf[bb]), reads=[ftok[bb]], writes=[f"Sjb{bb}"])

                def s_prep(j):
                    q2, k2, jb = qmb[j % 2], kmb[j % 2], j % 2
                    pg.op("dve", lambda e: e.memset(q2.rearrange("p k c -> p (k c)"), 0.0), writes=[f"qm{jb}"])
                    pg.op("dve", lambda e: e.tensor_copy(out=q2[:, :, 8 * j:8 * j + 8], in_=t_qkT[:, 0:2, 8 * j:8 * j + 8]),
                          reads=["qkT"], writes=[f"qm{jb}"])
                    pg.op("dve", lambda e: e.tensor_scalar(out=k2, in0=t_ke, scalar1=selr[:, j:j + 1], scalar2=None, op0=ALU.mult),
                          reads=["ke", "selr"], writes=[f"km{jb}"])

                def s_tail(i):
                    j, dc = items[i]
                    bb, pb2 = i % NB, i % 2
                    pg.op("dve", lambda e: e.tensor_tensor(out=fbuf[bb], in0=fbuf[bb], in1=pP[pb2], op=ALU.add),
                          reads=[ftok[bb], f"ps{5 + pb2}"], writes=[ftok[bb]])
                    col = dc * NS + j
                    pg.op("dve", lambda e: e.tensor_scalar(out=fbuf[bb], in0=fbuf[bb], scalar1=t_ebl[:, col:col + 1], scalar2=None, op0=ALU.mult),
                          reads=[ftok[bb], "ebl"], writes=[ftok[bb]])
                    pg.dma("act", o_sgs[j, h, dc * P:(dc + 1) * P, :], fbuf[bb], f"SS{bb}", reads=[ftok[bb]])

                for i0 in range(min(NB - 1, len(items))):
                    s_loads(i0)
                s_prep(0)
                s_cast(0)
                for i, (j, dc) in enumerate(items):
                    bb, pb2, jb = i % NB, i % 2, j % 2
                    if dc == 0 and j + 1 < NS:
                        s_prep(j + 1)
                    if i + 1 < len(items):
                        s_cast(i + 1)
                    last = (i == len(items) - 1)
                    pg.op("pe", lambda e, dc=dc, bb=bb, last=last, jb=jb: e.matmul(po, lhsT=qmb[jb][:, dc, :], rhs=t_Sjb[:, bb, :], start=False, stop=last),
                          reads=[f"qm{jb}", f"Sjb{bb}"], writes=["ps4"])
                    pg.op("pe", lambda e, dc=dc, pb2=pb2, jb=jb: e.matmul(pP[pb2], lhsT=kmb[jb][:, dc * P:(dc + 1) * P], rhs=v, start=True, stop=True),
                          reads=[f"km{jb}", ztok], writes=[f"ps{5 + pb2}"])
                    if i >= 1:
                        s_tail(i - 1)
                    if i + NB - 1 < len(items):
                        s_loads(i + NB - 1)
                    if i % 4 == 3:
                        yield
                s_tail(len(items) - 1)
                yield
            if not full:
                yield
                return
            ss, tmp, rstd = stat[:, 4:5], stat[:, 5:6], stat[:, 6:7]
            pg.op("dve", lambda e: e.memset(ss, 0.0), writes=["Gss"])
            pg.op("act", lambda e: e.activation(out=t_f3, in_=po, func=AF.Square, accum_out=ss), reads=["ps4"], writes=["Gss", "f3"])
            rstd_chain(ss, tmp, rstd, DV, "G")
            pg.op("dve", lambda e: e.scalar_tensor_tensor(out=t_ma, in0=po, scalar=rstd, in1=t_f2, op0=ALU.mult, op1=ALU.mult),
                  reads=["ps4", "Grstd", "f2"], writes=["ma"])
            pg.op("act", lambda e: e.activation(out=t_f3, in_=gb, func=AF.Exp, scale=-1.0), reads=[ztok], writes=["f3"])
            pg.op("act", lambda e: e.activation(out=t_f3, in_=t_f3, func=AF.Ln, bias=1.0), reads=["f3"], writes=["f3"])
            pg.op("act", lambda e: e.activation(out=t_f3, in_=t_f3, func=AF.Exp, scale=-1.0), reads=["f3"], writes=["f3"])
            pg.op("dve", lambda e: e.tensor_tensor(out=t_f3, in0=t_f3, in1=pscale, op=ALU.mult), reads=["f3", "pscale"], writes=["f3"])
            yield
            g = h
            ppt = ps[7][:, 0:256]
            if kind == "P":
                bcur = CM((C_BFIRST if band_first else C_BCUR) + g)

                def mmpool(e):
                    inst = None
                    for cc in range(2):
                        e.matmul(ppt[:, cc * P:(cc + 1) * P], lhsT=uprev[:, cc * P:(cc + 1) * P], rhs=CM(C_BPREV + g), start=True, stop=False)
                        inst = e.matmul(ppt[:, cc * P:(cc + 1) * P], lhsT=u[:, cc * P:(cc + 1) * P], rhs=bcur, start=False, stop=True)
                    return inst
                pg.op("pe", mmpool, reads=[ztok, uprev_tok, "cm"], writes=["ps7"])
            else:
                def mmpool(e):
                    inst = None
                    for cc in range(2):
                        c0 = g * 256 + cc * P
                        e.matmul(ppt[:, cc * P:(cc + 1) * P], lhsT=bufS[0:120, 0, c0:c0 + P], rhs=cm[0:120, C_BSB + 2 * g, :], start=True, stop=False)
                        e.matmul(ppt[:, cc * P:(cc + 1) * P], lhsT=bufS[0:120, 1, c0:c0 + P], rhs=cm[0:120, C_BSB + 2 * g + 1, :], start=False, stop=False)
                        inst = e.matmul(ppt[:, cc * P:(cc + 1) * P], lhsT=u[:, cc * P:(cc + 1) * P], rhs=CM(C_BSC + g), start=False, stop=True)
                    return inst
                pg.op("pe", mmpool, reads=[ztok, "bufS", "cm"], writes=["ps7"])
            pg.op("act", lambda e: e.copy(out=t_pT.rearrange("p k c -> p (k c)"), in_=ppt), reads=["ps7"], writes=["pT"])
            yield
            def mmob(e):
                e.matmul(po, lhsT=t_pT[:, 0, :], rhs=poolw[:, 0, :], start=True, stop=False)
                return e.matmul(po, lhsT=t_pT[:, 1, :], rhs=poolw[:, 1, :], start=False, stop=True)
            pg.op("pe", mmob, reads=["pT", "poolw"], writes=["ps4"])
            pg.op("dve", lambda e: e.tensor_tensor(out=t_f3, in0=po, in1=t_f3, op=ALU.mult), reads=["ps4", "f3"], writes=["f3"])
            pg.op("dve", lambda e: e.tensor_tensor(out=t_m, in0=t_ma, in1=t_f3, op=ALU.add), reads=["ma", "f3"], writes=["m"])
            yield
            ptm = psb(7, 256, 512).rearrange("p (k c) -> p k c", k=4)

            def trm(e):
                inst = None
                for j in range(4):
                    inst = e.transpose(ptm[:, j, :], t_m[:, j * P:(j + 1) * P], CM(C_ID))
                return inst
            pg.op("pe", trm, reads=["m", "cm"], writes=["ps7"])
            dst = mT[:, h * 4:(h + 1) * 4, tile_glob * P:(tile_glob + 1) * P]
            pg.op("act", lambda e: e.copy(out=dst, in_=ptm), reads=["ps7"], writes=["mT"])
            yield

        def zip_steps(fronts, backs):
            if not fronts:
                return
            for _ in fronts[0]():
                yield
            for n in range(len(backs)):
                b = backs[n]()
                f = fronts[n + 1]() if n + 1 < len(fronts) else None
                while b is not None or f is not None:
                    if b is not None:
                        try:
                            next(b)
                            yield
                        except StopIteration:
                            b = None
                    if f is not None:
                        try:
                            next(f)
                            yield
                        except StopIteration:
                            f = None

        s_init = [False] * H

        def state_in(h):
            if not s_init[h]:
                s_init[h] = True
                pg.op("dve", lambda e: e.memset(S32.rearrange("p c v -> p (c v)"), 0.0), writes=["S32"])
            else:
                pg.dma("sp", S32, o_sgp[h].rearrange("(c p) v -> p c v", p=P), "dSl", reads=[f"dramS{h}"], writes=["S32"])

        def state_out(h):
            pg.dma("sp", o_sgp[h].rearrange("(c p) v -> p c v", p=P), S32, "dSs", reads=["S32"], writes=[f"dramS{h}"])

        def n_stages(kind, full):
            if not full:
                return 3
            return 8 + ((NS // 2 + 1) if kind == "S" else 0)

        def finish():
            nops, nwaits = pg.emit()
            print(f"[kernel] ops={nops} waits={nwaits} sbuf_peak={A.peak} sems={len(pg.sems)}")

        def wsrc(bid, ncols):
            return w_in_blk[bid]

        def interleave(chunks, chain, nst):
            left = nst
            for i, ch in enumerate(chunks):
                ch()
                if chain is not None and left > 0:
                    k = -(-left // (len(chunks) - i))
                    for _ in range(k):
                        try:
                            next(chain)
                        except StopIteration:
                            chain = None
                            left = 0
                            break
                        left -= 1
            if chain is not None:
                for _ in chain:
                    pass

        def dense_chunks(slots, colspecs, dst_of, ztoks, extra_of=None, after_block=None):
            out = []
            for bi, (c0, zoff) in enumerate(colspecs):
                state = {}
                for n, sl in enumerate(slots):
                    def ch(bi=bi, c0=c0, zoff=zoff, n=n, sl=sl, state=state):
                        if n == 0:
                            state["s"] = load_w(lambda s: wbuf[s], wsrc(c0, WB))
                        s = state["s"]
                        ex = (lambda _n, n=n: extra_of(bi, n)) if extra_of is not None else None
                        dense_block(hT, [sl], s, wbuf[s], WB, lambda _n, n=n, zoff=zoff: dst_of(n, zoff), [ztoks[n]], extra_fp32=ex)
                        if after_block is not None:
                            after_block(bi, n, s)
                    out.append(ch)
            return out

        d0_done = {}
        p_done_pending = {}
        blocks = [("q", ZQ, lambda h: h * 256), ("k", ZK, lambda h: 1024 + h * 256),
                  ("v0", ZV, lambda h: 2048 + h * 512), ("v1", ZV + 256, lambda h: 2048 + h * 512 + 256),
                  ("r0", ZR, lambda h: 4096 + h * 512), ("r1", ZR + 256, lambda h: 4096 + h * 512 + 256),
                  ("ga0", ZGA, lambda h: 7184 + h * 512), ("ga1", ZGA + 256, lambda h: 7184 + h * 512 + 256),
                  ("gb0", ZGB, lambda h: 9232 + h * 512), ("gb1", ZGB + 256, lambda h: 9232 + h * 512 + 256),
                  ("u", ZU, lambda h: 6160 + h * 256)]
        UBI = len(blocks) - 1
        a_done = {}
        PRE_LAST_SLOT = NSLOT_H - 1
        PG = min(NPRE, NSLOT_H + N_XS) if NPRE > 0 else 1
        if NPRE > PG:
            PG = min(4, cfg.ntmax)

        def _pre_slot(ti, g0):
            if ti == NPRE - 1:
                return PRE_LAST_SLOT
            r = ti - g0
            return r if r < NSLOT_H - 1 else r + 1
        def make_dense(sbi, h):
            tiles = cfg.sbs[sbi]
            nts = len(tiles)
            slots = list(range(nts))
            zs = h % 2
            zb = zbuf[zs]
            ztoks = [f"z{zs}_{n}" for n in range(nts)]
            specs = [(h * 11 + bi_, zoff) for bi_, (nm, zoff, cfn) in enumerate(blocks)]

            def extra_of(bi, n):
                if bi != UBI:
                    return None
                tg = tiles[n]
                if tg == NMAIN - 1:
                    return (t_u32, "u32")
                if tg == NT - 1:
                    return (t_u32s, "u32")
                return None

            def after_block(bi, n, s):
                if bi != UBI:
                    return
                tg = tiles[n]
                if tg == NMAIN - 1:
                    pg.dma("sp", o_plp[:, h * 256:(h + 1) * 256], t_u32[P - 15:P, :], "oplu", reads=["u32"])
                if tg == NT - 1:
                    for j in range(NS):
                        pg.dma("sp", o_pls[j, 7:15, h * 256:(h + 1) * 256], t_u32s[8 * j:8 * j + 8, :], "oplu", reads=["u32"])
                if n == len(tiles) - 1 and sbi == 0 and NPRE > 0:
                    dense_block(hT, [PRE_LAST_SLOT], s, wbuf[s], WB, lambda n_: ucarry[:, h, :], [f"uc{h}"])

            return dense_chunks(slots, specs, lambda n, zoff: zb[:, n, zoff:zoff + WB], ztoks,
                                extra_of=extra_of, after_block=after_block)

        def make_chain(sbi, h):
            tiles = cfg.sbs[sbi]
            nts = len(tiles)
            zs = h % 2
            zb = zbuf[zs]
            ztoks = [f"z{zs}_{n}" for n in range(nts)]

            def gen():
                pg.dma("pool", poolw, pool_w[h].rearrange("(c p) v -> p c v", p=P), "pw", writes=["poolw"])
                pg.dma("sp", pscale, pool_scale[h * 512:(h + 1) * 512].partition_broadcast(P), "psc", writes=["pscale"])
                state_in(h)
                pg.op("act", lambda e: e.copy(out=Sbf.rearrange("p c v -> p (c v)"), in_=S32.rearrange("p c v -> p (c v)")),
                      reads=["S32"], writes=["Sbf"])
                fr, bk = [], []
                last_prompt_n = None
                for n, tg in enumerate(tiles):
                    zt = zb[:, n, :]
                    V = {"q": zt[:, ZQ:ZQ + 256], "k": zt[:, ZK:ZK + 256], "v": zt[:, ZV:ZV + 512], "r": zt[:, ZR:ZR + 512],
                         "ga": zt[:, ZGA:ZGA + 512], "gb": zt[:, ZGB:ZGB + 512], "u": zt[:, ZU:ZU + 256]}
                    if tg < NMAIN:
                        if n == 0:
                            up, upt = ucarry[:, h, :], f"uc{h}"
                        else:
                            up, upt = zb[:, n - 1, ZU:ZU + 256], ztoks[n - 1]
                        kw = dict(uprev=up, uprev_tok=upt, band_first=(tg == 0), tile_glob=tg)
                        fr.append(lambda n=n, V=V, kw=kw: gla_step(h, V, ztoks[n], n, "P", True, part="front", **kw))
                        bk.append(lambda n=n, V=V, kw=kw: gla_step(h, V, ztoks[n], n, "P", True, part="back", **kw))
                        last_prompt_n = n
                    else:
                        fr.append(lambda n=n, V=V, tg=tg: gla_step(h, V, ztoks[n], n, "S", True, tile_glob=tg, part="front"))
                        bk.append(lambda n=n, V=V, tg=tg: gla_step(h, V, ztoks[n], n, "S", True, tile_glob=tg, part="back"))
                yield from zip_steps(fr, bk)
                state_out(h)
                if last_prompt_n is not None and sbi + 1 < len(cfg.sbs):
                    pg.op("act", lambda e, n=last_prompt_n: e.copy(out=ucarry[:, h, :], in_=zb[:, n, ZU:ZU + 256]),
                          reads=[ztoks[last_prompt_n]], writes=[f"uc{h}"])
            nst = sum(n_stages("P" if tg < NMAIN else "S", True) for tg in tiles)
            return gen(), nst

        def merge_tilewise(chsA, dch, nt):
            out = list(chsA[:2])
            ai = 2
            for n in range(nt):
                out.append(dch[n])
                if ai < len(chsA):
                    out.append(chsA[ai])
                    ai += 1
            out += chsA[ai:]
            out += dch[nt:]
            return out

        def with_after(fn, tag):
            def g():
                prev = pg.cur_after
                pg.cur_after = tag
                try:
                    fn()
                finally:
                    pg.cur_after = prev
            return g

        for g0 in range(0, NPRE, PG):
            ptiles = list(range(g0, min(NPRE, g0 + PG)))
            slot_of = lambda ti, g0=g0: _pre_slot(ti, g0)
            pslots = [slot_of(ti) for ti in ptiles]
            npt = len(ptiles)
            chsA0 = phase_A_chunks(xpre, ptiles, slot_of, gbmix, xt, xn, "P")
            chain, nst = None, 0
            for h in range(H):
                zs = h % 2
                zflat = zbuf[zs].rearrange("p n c -> p (n c)")
                zp = zflat[:, 0:npt * 768].rearrange("p (n c) -> p n c", n=npt)
                ztoks = [f"z{zs}_{n}" for n in range(npt)]
                specs = [(h * 11 + 1, 0), (h * 11 + 2, 256), (h * 11 + 3, 512)]
                chunks = dense_chunks(pslots, specs, lambda n, zoff, zp=zp: zp[:, n, zoff:zoff + WB], ztoks)
                if h == 0:
                    for c_ in merge_tilewise(chsA0, chunks, npt):
                        c_()
                    pg.fence()
                    alr_block(pslots)
                else:
                    interleave(chunks, chain, nst)

                def mk_chain(h=h, zp=zp, ztoks=ztoks, npt=npt):
                    state_in(h)
                    fr, bk = [], []
                    for n in range(npt):
                        Vn = {"k": zp[:, n, 0:256], "v": zp[:, n, 256:768]}
                        fr.append(lambda n=n, Vn=Vn: gla_step(h, Vn, ztoks[n], n, "P", False, part="front"))
                        bk.append(lambda n=n, Vn=Vn: gla_step(h, Vn, ztoks[n], n, "P", False, part="back"))
                    yield from zip_steps(fr, bk)
                    state_out(h)
                chain, nst = mk_chain(), 3 * npt
            if CAN_OVL and g0 + PG >= NPRE:
                pg.snapshot("tail")
                t0_ = cfg.sbs[0]
                chs = phase_A_chunks(xmain, t0_, lambda ti, t0=t0_[0]: ti - t0, gbmixB, xtB, [xnB], "Q", after="tail")
                chs.append(lambda: pg.snapshot("Adone"))
                chs += [with_after(c, "Adone") for c in make_dense(0, 0)]
                interleave(chs, chain, nst)
                pg.snapshot("P_done")
                a_done[0] = True
                d0_done[0] = True
                p_done_pending[0] = True
            else:
                interleave([], chain, nst)
                pg.fence()

        for sbi, tiles in enumerate(cfg.sbs):
            nts = len(tiles)
            slots = list(range(nts))
            if not a_done.get(sbi):
                phase_A(xmain, tiles, lambda ti, t0=tiles[0]: ti - t0)
                pg.fence()
            alr_block(slots)
            chain, nst = None, 0
            for h in range(H):
                if h == 0 and d0_done.get(sbi):
                    pass
                else:
                    dch = make_dense(sbi, h)
                    if p_done_pending.get(sbi):
                        dch = [with_after(c, "P_done") for c in dch]
                    interleave(dch, chain, nst)
                if sbi == 0 and NPRE == 0:
                    pg.op("dve", lambda e, h=h: e.memset(ucarry[:, h, :], 0.0), writes=[f"uc{h}"])
                chain, nst = make_chain(sbi, h)
            if CAN_OVL and sbi + 1 < len(cfg.sbs):
                pg.snapshot("tail")
                t1_ = cfg.sbs[sbi + 1]
                chs = phase_A_chunks(xmain, t1_, lambda ti, t0=t1_[0]: ti - t0, gbmixB, xtB, [xnB], "Q", after="tail")
                chs.append(lambda: pg.snapshot("Adone"))
                chs += [with_after(c, "Adone") for c in make_dense(sbi + 1, 0)]
                interleave(chs, chain, nst)
                a_done[sbi + 1] = True
                d0_done[sbi + 1] = True
            else:
                interleave([], chain, nst)
                pg.fence()

        A.top = mark_B
        acc = A.alloc(KC * NTOK, F32).rearrange("p (k t) -> p k t", k=KC)
        h2T = A.alloc(KC * NTOK, BF16).rearrange("p (k t) -> p k t", k=KC)
        wb2 = [A.alloc(KC * WB, BF16).rearrange("p (k c) -> p k c", k=KC) for _ in range(2)]
        sqb = [A.alloc(512, BF16) for _ in range(2)]
        rrow = A.alloc(NTOK, F32)
        rbc = A.alloc(NTOK, F32)
        gT = A.alloc(2 * KC, F32)
        xt2 = [A.alloc(D, F32) for _ in range(2)]
        NFC = FSB // P
        actT = mT_flat[:, 0:NFC * NTOK].rearrange("p (k t) -> p k t", k=NFC)
        relu_t = [A.alloc(512, BF16) for _ in range(2)]
        tbs = [(t0, min(512, NTOK - t0)) for t0 in range(0, NTOK, 512)]

        pg.dma("sp", gT[:, 0:KC], norm_mlp_g.rearrange("(k p) -> p k", p=P), "gT", writes=["gT"], nc_ok=True)
        pg.dma("sp", gT[:, KC:2 * KC], norm_final_g.rearrange("(k p) -> p k", p=P), "gT", writes=["gT"], nc_ok=True)

        w2ctr = [0]

        def load_w2(view_fn, src):
            s = w2ctr[0] % 2
            w2ctr[0] += 1
            pg.dma("pool", view_fn(wb2[s]), src, f"v{s}", writes=[f"v{s}"])
            return s

        pctr = [0]

        def next_ps():
            pctr[0] += 1
            return pctr[0] % 4

        pg.op("dve", lambda e: e.memset(acc.rearrange("p k t -> p (k t)"), 0.0), writes=[f"acc{kc}" for kc in range(KC)])
        c2_chunks = []
        for ob in range(D // WB):
            st_ = {}
            for occ in range(WB // P):
                oc = ob * (WB // P) + occ
                for (t0, tn) in tbs:
                    def c2(ob=ob, occ=occ, oc=oc, t0=t0, tn=tn, st_=st_):
                        if "s" not in st_:
                            st_["s"] = load_w2(lambda w: w, w_out_blk[ob])
                        s = st_["s"]
                        pb = next_ps()
                        pz = ps[pb][:, 0:tn]

                        def mm(e):
                            inst = None
                            for kc in range(KC):
                                inst = e.matmul(pz, lhsT=wb2[s][:, kc, occ * P:(occ + 1) * P], rhs=mT[:, kc, t0:t0 + tn],
                                                start=(kc == 0), stop=(kc == KC - 1))
                            return inst
                        pg.op("pe", mm, reads=[f"v{s}", "mT"], writes=[f"ps{pb}"])
                        dst = acc[:, oc, t0:t0 + tn]
                        pg.op("dve", lambda e: e.tensor_tensor(out=dst, in0=dst, in1=pz, op=ALU.add),
                              reads=[f"ps{pb}", f"acc{oc}"], writes=[f"acc{oc}"])
                    c2_chunks.append(c2)
        c1_units = []
        for ti in range(NT):
            for g4 in range(4):
                def c1(ti=ti, g4=g4):
                    b = ti % 2
                    if g4 == 0:
                        pg.dma("sp", xt2[b], xmain[ti * P:(ti + 1) * P, :], f"x2{b}", writes=[f"x2{b}"])
                    pb = 4 + (g4 % 2)
                    pt = ps[pb].rearrange("p (k c) -> p k c", k=4)

                    def tr(e):
                        inst = None
                        for j in range(4):
                            kc = g4 * 4 + j
                            inst = e.transpose(pt[:, j, :], xt2[b][:, kc * P:(kc + 1) * P], identF)
                        return inst
                    pg.op("pe", tr, reads=[f"x2{b}", "identF"], writes=[f"ps{pb}"])
                    dst = acc[:, g4 * 4:(g4 + 1) * 4, ti * P:(ti + 1) * P]
                    toks = [f"acc{g4 * 4 + j}" for j in range(4)]
                    pg.op("dve", lambda e: e.tensor_tensor(out=dst, in0=dst, in1=pt, op=ALU.add),
                          reads=[f"ps{pb}"] + toks, writes=toks)
                c1_units.append(c1)
        left = len(c1_units)
        ui = 0
        for i, ch in enumerate(c2_chunks):
            ch()
            k = -(-left // (len(c2_chunks) - i)) if left > 0 else 0
            for _ in range(k):
                c1_units[ui]()
                ui += 1
                left -= 1
        while ui < len(c1_units):
            c1_units[ui]()
            ui += 1

        def rms_rows(tag):
            acct = [f"acc{kc}" for kc in range(KC)]
            for (t0, tn) in tbs:
                psr = ps[6][0:1, 0:tn]
                for kc in range(KC):
                    sb_ = sqb[kc % 2]
                    if kc % 2 == 0:
                        pg.op("act", lambda e, kc=kc, sb_=sb_, t0=t0, tn=tn: e.activation(out=sb_[:, 0:tn], in_=acc[:, kc, t0:t0 + tn], func=AF.Square),
                              reads=[f"acc{kc}"], writes=[f"sq{kc % 2}"])
                    else:
                        pg.op("dve", lambda e, kc=kc, sb_=sb_, t0=t0, tn=tn: e.tensor_tensor(out=sb_[:, 0:tn], in0=acc[:, kc, t0:t0 + tn],
                                                                                          in1=acc[:, kc, t0:t0 + tn], op=ALU.mult),
                              reads=[f"acc{kc}"], writes=[f"sq{kc % 2}"])
                    pg.op("pe", lambda e, kc=kc, sb_=sb_, tn=tn, psr=psr: e.matmul(psr, lhsT=ones_b[:, 0:1], rhs=sb_[:, 0:tn],
                                                                                  start=(kc == 0), stop=(kc == KC - 1)),
                          reads=[f"sq{kc % 2}", "ones_b"], writes=["ps6"])
                rr = rrow[0:1, t0:t0 + tn]
                pg.op("dve", lambda e, rr=rr, psr=psr: e.tensor_scalar(out=rr, in0=psr, scalar1=1.0 / D, scalar2=EPS, op0=ALU.mult, op1=ALU.add),
                      reads=["ps6"], writes=["rrow"])
                pg.op("act", lambda e, rr=rr: e.activation(out=rr, in_=rr, func=AF.Ln), reads=["rrow"], writes=["rrow"])
                pg.op("act", lambda e, rr=rr: e.activation(out=rr, in_=rr, func=AF.Exp, scale=-0.5), reads=["rrow"], writes=["rrow"])
                pbc = ps[7][:, 0:tn]
                pg.op("pe", lambda e, rr=rr, pbc=pbc: e.matmul(pbc, lhsT=onesF[0:1, :], rhs=rr, start=True, stop=True),
                      reads=["rrow", "onesF"], writes=["ps7"])
                pg.op("act", lambda e, pbc=pbc, t0=t0, tn=tn: e.copy(out=rbc[:, t0:t0 + tn], in_=pbc), reads=["ps7"], writes=["rbc"])

        rms_rows("mlp")
        for kc in range(KC):
            pg.op("dve", lambda e, kc=kc: e.scalar_tensor_tensor(out=h2T[:, kc, :], in0=acc[:, kc, :], scalar=gT[:, kc:kc + 1], in1=rbc,
                                                                   op0=ALU.mult, op1=ALU.mult),
                  reads=[f"acc{kc}", "gT", "rbc"], writes=["h2T"])
        pg.fence()

        for f in range(FF // FSB):
            for ub in range(FSB // WB):
                c0 = f * FSB + ub * WB
                s = load_w2(lambda w: w, w_up_blk[c0 // WB])
                for fc in range(WB // P):
                    ffc = ub * (WB // P) + fc
                    for (t0, tn) in tbs:
                        pb = next_ps()
                        pz = ps[pb][:, 0:tn]

                        def mm(e, s=s, fc=fc, t0=t0, tn=tn, pz=pz):
                            inst = None
                            for kc in range(KC):
                                inst = e.matmul(pz, lhsT=wb2[s][:, kc, fc * P:(fc + 1) * P], rhs=h2T[:, kc, t0:t0 + tn],
                                                start=(kc == 0), stop=(kc == KC - 1))
                            return inst
                        pg.op("pe", mm, reads=[f"v{s}", "h2T"], writes=[f"ps{pb}"])
                        rt = relu_t[pb % 2]
                        pg.op("act", lambda e, rt=rt, pz=pz, tn=tn: e.activation(out=rt[:, 0:tn], in_=pz, func=AF.Relu),
                              reads=[f"ps{pb}"], writes=[f"relu{pb % 2}"])
                        dst = actT[:, ffc, t0:t0 + tn]
                        pg.op("dve", lambda e, rt=rt, dst=dst, tn=tn: e.tensor_tensor(out=dst, in0=rt[:, 0:tn], in1=rt[:, 0:tn], op=ALU.mult),
                              reads=[f"relu{pb % 2}"], writes=[f"actT{ffc}"])
            for ob in range(D // WB):
                s = load_w2(lambda w: w[:, 0:NFC, :], w_down_blk[f * (D // WB) + ob])
                for occ in range(WB // P):
                    oc = ob * (WB // P) + occ
                    for (t0, tn) in tbs:
                        pb = next_ps()
                        pz = ps[pb][:, 0:tn]

                        def mm(e, s=s, occ=occ, t0=t0, tn=tn, pz=pz):
                            inst = None
                            for k in range(NFC):
                                inst = e.matmul(pz, lhsT=wb2[s][:, k, occ * P:(occ + 1) * P], rhs=actT[:, k, t0:t0 + tn],
                                                start=(k == 0), stop=(k == NFC - 1))
                            return inst
                        pg.op("pe", mm, reads=[f"v{s}"] + [f"actT{k}" for k in range(NFC)], writes=[f"ps{pb}"])
                        dst = acc[:, oc, t0:t0 + tn]
                        pg.op("dve", lambda e, dst=dst, pz=pz: e.tensor_tensor(out=dst, in0=dst, in1=pz, op=ALU.add),
                              reads=[f"ps{pb}", f"acc{oc}"], writes=[f"acc{oc}"])
        pg.fence()

        rms_rows("fin")
        for kc in range(KC):
            pg.op("dve", lambda e, kc=kc: e.scalar_tensor_tensor(out=acc[:, kc, :], in0=acc[:, kc, :], scalar=gT[:, KC + kc:KC + kc + 1], in1=rbc,
                                                                   op0=ALU.mult, op1=ALU.mult),
                  reads=[f"acc{kc}", "gT", "rbc"], writes=[f"acc{kc}"])
        for ti in range(NT):
            b = ti % 2
            yt = xt2[b]
            for g4 in range(4):
                pb = g4 % 4
                pt = ps[pb].rearrange("p (k c) -> p k c", k=4)

                def tr(e, g4=g4, pt=pt, ti=ti):
                    inst = None
                    for j in range(4):
                        kc = g4 * 4 + j
                        inst = e.transpose(pt[:, j, :], acc[:, kc, ti * P:(ti + 1) * P], identF)
                    return inst
                pg.op("pe", tr, reads=[f"acc{g4 * 4 + j}" for j in range(4)] + ["identF"], writes=[f"ps{pb}"])
                dst = yt[:, g4 * 512:(g4 + 1) * 512]
                if g4 % 2 == 0:
                    pg.op("act", lambda e, dst=dst, pb=pb: e.copy(out=dst, in_=ps[pb]), reads=[f"ps{pb}"], writes=[f"x2{b}"])
                else:
                    pg.op("dve", lambda e, dst=dst, pb=pb: e.tensor_copy(out=dst, in_=ps[pb]), reads=[f"ps{pb}"], writes=[f"x2{b}"])
            pg.dma("sp", y[ti * P:(ti + 1) * P, :], yt, f"y{b}", reads=[f"x2{b}"])

        finish()
    return nc


def make_consts(first_half):
    c = np.zeros((NCONST, P, P), np.float32)
    idx = np.arange(P)
    s_, t_ = idx[:, None], idx[None, :]
    c[C_ID] = np.eye(P)
    tri = (s_ <= t_).astype(np.float32)
    same = ((s_ // 8) == (t_ // 8)).astype(np.float32)
    c[C_MASKP] = tri
    c[C_CUMP] = -tri / 16.0
    c[C_MASKS] = tri * same
    c[C_CUMS] = -tri * same / 16.0
    for g, w in enumerate((2, 4, 8, 16)):
        eye = np.eye(P, dtype=np.float32)
        cur = ((s_ <= t_) & (s_ > t_ - w)).astype(np.float32) / w - eye
        c[C_BCUR + g] = cur
        c[C_BPREV + g] = ((s_ - P) > (t_ - w)).astype(np.float32) / w
        if first_half:
            cnt = np.minimum(t_ + 1, w).astype(np.float32)
            c[C_BFIRST + g] = ((s_ <= t_) & (s_ > t_ - w)).astype(np.float32) / cnt - eye
        else:
            c[C_BFIRST + g] = cur
        i_s, i_t = s_ % 8, t_ % 8
        c[C_BSC + g] = (same * ((i_s <= i_t) & (i_s > i_t - w))).astype(np.float32) / w - eye
        for hf in range(2):
            rows = np.arange(120)
            jr = rows // 15 + 8 * hf
            r = rows % 15
            m = (jr[:, None] == (idx[None, :] // 8)) & ((r[:, None] - 15) > ((idx[None, :] % 8) - w))
            c[C_BSB + 2 * g + hf, 0:120, :] = m.astype(np.float32) / w
    selr = ((idx[:, None] // 8) == np.arange(16)[None, :]).astype(np.float32)
    return c, selr


def in_block_starts():
    st = []
    for h in range(H):
        st += [h * 256, 1024 + h * 256, 2048 + h * 512, 2048 + h * 512 + 256, 4096 + h * 512, 4096 + h * 512 + 256,
               7184 + h * 512, 7184 + h * 512 + 256, 9232 + h * 512, 9232 + h * 512 + 256, 6160 + h * 256]
    return st


def relayout_weights(w_in, w_out, w_up, w_down, cfg):
    def colblk(w, c0, width=WB):
        kc = w.shape[0] // P
        return w[:, c0:c0 + width].reshape(kc, P, width).transpose(1, 0, 2)
    out = {}
    out["w_in_blk"] = np.ascontiguousarray(np.stack([colblk(w_in, c0) for c0 in in_block_starts()]))
    out["w_alr"] = np.ascontiguousarray(colblk(w_in, 6144, 16))
    out["w_out_blk"] = np.ascontiguousarray(np.stack([colblk(w_out, c0) for c0 in range(0, D, WB)]))
    out["w_up_blk"] = np.ascontiguousarray(np.stack([colblk(w_up, c0) for c0 in range(0, cfg.ff, WB)]))
    dn = []
    for f_ in range(cfg.ff // cfg.fsb):
        rows = w_down[f_ * cfg.fsb:(f_ + 1) * cfg.fsb]
        for c0 in range(0, D, WB):
            dn.append(colblk(rows, c0))
    out["w_down_blk"] = np.ascontiguousarray(np.stack(dn))
    return out


_NC_CACHE = {}


def _get_nc(cfg_key, cfg):
    if cfg_key not in _NC_CACHE:
        _NC_CACHE[cfg_key] = build_program(cfg)
    return _NC_CACHE[cfg_key]


def kernel(x_prompt, x_sample, state_gla, state_pool, norm_mix_g, w_in, w_alpha_up, b_alpha, gla_norm_g,
           pool_w, pool_scale, w_out, norm_mlp_g, w_up, w_down, norm_final_g):
    f = lambda a: np.ascontiguousarray(np.asarray(a, dtype=np.float32))
    x_prompt, x_sample, state_gla, state_pool = f(x_prompt), f(x_sample), f(state_gla), f(state_pool)
    B, T, _ = x_prompt.shape
    NCORES = 8
    half = T // 2
    cfg = Cfg()
    nc = _get_nc("full", cfg)
    shared = {"norm_mix_g": f(norm_mix_g), "w_alpha_up": f(w_alpha_up), "b_alpha": f(b_alpha),
              "gla_norm_g": f(gla_norm_g), "pool_w": f(pool_w), "pool_scale": f(pool_scale),
              "norm_mlp_g": f(norm_mlp_g), "norm_final_g": f(norm_final_g)}
    shared.update(relayout_weights(f(w_in), f(w_out), f(w_up), f(w_down), cfg))
    consts = [make_consts(True), make_consts(False)]
    zeros_pre = np.zeros((half, D), np.float32)
    in_maps = []
    for c in range(NCORES):
        b, hf = c // 2, c % 2
        xs = x_sample[c * 16:(c + 1) * 16].reshape(16 * 8, D)
        xmain = np.concatenate([x_prompt[b, hf * half:(hf + 1) * half], xs], axis=0)
        xpre = zeros_pre if hf == 0 else x_prompt[b, 0:half]
        cm, selr = consts[hf]
        m = dict(shared)
        m.update({"xpre": np.ascontiguousarray(xpre), "xmain": np.ascontiguousarray(xmain),
                  "sgla": state_gla[c * 16:(c + 1) * 16], "spool": state_pool[c * 16:(c + 1) * 16],
                  "cmat": cm, "selr": selr})
        in_maps.append(m)
    res = run_bass_kernel_spmd(nc, in_maps, core_ids=list(range(NCORES)))
    R = res.results
    y_p = np.empty((B, T, D), np.float32)
    y_s = np.empty((128, 8, D), np.float32)
    sg_p = np.empty((B, H, DK, DV), np.float32)
    pl_p = np.empty((B, 15, 1024), np.float32)
    sg_s = np.empty((128, H, DK, DV), np.float32)
    pl_s = np.empty((128, 15, 1024), np.float32)
    for c in range(NCORES):
        b, hf = c // 2, c % 2
        yc = R[c]["y"]
        y_p[b, hf * half:(hf + 1) * half] = yc[0:half]
        y_s[c * 16:(c + 1) * 16] = yc[half:half + 128].reshape(16, 8, D)
        if hf == 1:
            sg_p[b] = R[c]["o_sgp"]
            pl_p[b] = R[c]["o_plp"]
        sg_s[c * 16:(c + 1) * 16] = R[c]["o_sgs"]
        pl_s[c * 16:(c + 1) * 16] = R[c]["o_pls"]
    return (y_p, y_s, sg_p, pl_p, sg_s, pl_s)
```

```python
import contextlib
import numpy as np
import concourse.bass as bass
import concourse.mybir as mybir
from concourse.bass_utils import run_bass_kernel_spmd

F32 = mybir.dt.float32
BF16 = mybir.dt.bfloat16
AF = mybir.ActivationFunctionType
ALU = mybir.AluOpType

P = 128
D = 2048
KC = D // P
H = 4
DK = 256
DV = 512
IN_W = 11280
EPS = 1e-6
LN_QSCALE = float(np.log(DK ** -0.5))
WB = 256
HCOLS = 2816
ZQ, ZK, ZV, ZR, ZGA, ZGB, ZU = 0, 256, 512, 1024, 1536, 2048, 2560

C_ID, C_MASKP, C_CUMP, C_MASKS, C_CUMS = 0, 1, 2, 3, 4
C_BPREV, C_BCUR, C_BFIRST, C_BSC, C_BSB = 5, 9, 13, 17, 21
C_REVS = 29
NCONST = 30


class Cfg:
    def __init__(self, npre=8, nmain=8, ff=8192, fsb=1024, sbs=None, ntmax=None):
        self.npre = npre
        self.nmain = nmain
        self.nt = nmain + 1
        self.ntok = self.nt * P
        self.ff = ff
        self.fsb = fsb
        self.sbs = sbs or [[0, 1, 2, 3, 4], [5, 6, 7, 8]]
        self.ntmax = max(len(s) for s in self.sbs)
        self.ntmax = max(self.ntmax, 1, ntmax or 1)


class Prog:
    COMPUTE = ("pe", "act", "dve", "pool")

    def __init__(self, nc, es):
        self.nc = nc
        self.es = es
        self.eng = {"pe": nc.tensor, "act": nc.scalar, "dve": nc.vector, "pool": nc.gpsimd, "sp": nc.sync}
        self.ops = []
        self.last_w = {}
        self.readers = {}
        self.dma_cnt = {}
        self.sems = {}
        self.fence_ops = {}
        self.snaps = {}
        self.cur_after = None

    def _sem(self, key):
        if key not in self.sems:
            self.sems[key] = self.es.enter_context(self.nc.semaphore("s_" + key))
        return self.sems[key]

    def _add(self, rec, reads, writes):
        oid = len(self.ops)
        deps = set()
        for t in reads:
            w = self.last_w.get(t)
            if w is not None:
                deps.add(w)
        for t in writes:
            w = self.last_w.get(t)
            if w is not None:
                deps.add(w)
            for r in self.readers.get(t, ()):
                deps.add(r)
        f = self.fence_ops.pop(rec["eng"], None)
        if f:
            deps.update(f)
        if self.cur_after is not None:
            deps.update(self.snaps[self.cur_after])
        rec["deps"] = deps
        rec["signal"] = False
        self.ops.append(rec)
        for t in reads:
            self.readers.setdefault(t, []).append(oid)
        for t in writes:
            self.last_w[t] = oid
            self.readers[t] = []
        return oid

    def op(self, eng, fn, reads=(), writes=()):
        return self._add({"eng": eng, "fn": fn, "kind": "c"}, list(reads), list(writes))

    def dma(self, q, out, in_, key, reads=(), writes=(), nc_ok=False):
        self.dma_cnt[key] = self.dma_cnt.get(key, 0) + 16
        rec = {"eng": q, "kind": "d", "out": out, "in": in_, "key": key, "ticket": self.dma_cnt[key], "nc_ok": nc_ok}
        return self._add(rec, list(reads), list(writes))

    def fence(self):
        last = {}
        for i, o in enumerate(self.ops):
            last[o["eng"]] = i
        dmas = [i for i, o in enumerate(self.ops) if o["kind"] == "d" and not o.get("fenced")]
        for i in dmas:
            self.ops[i]["fenced"] = True
        allp = set(last.values()) | set(dmas)
        for e in self.eng:
            prev = self.fence_ops.get(e, set())
            self.fence_ops[e] = set(prev) | allp

    def snapshot(self, tag):
        last = {}
        for i, o in enumerate(self.ops):
            last[o["eng"]] = i
        dmas = [i for i, o in enumerate(self.ops) if o["kind"] == "d" and not o.get("fenced")]
        self.snaps[tag] = set(last.values()) | set(dmas)

    def emit(self):
        ops = self.ops
        for o in ops:
            for d in o["deps"]:
                od = ops[d]
                if od["kind"] == "c":
                    if od["eng"] == o["eng"] and o["eng"] == "pe":
                        continue
                    od["signal"] = True
        cnt = {e: 0 for e in self.COMPUTE}
        for o in ops:
            if o["kind"] == "c" and o["signal"]:
                cnt[o["eng"]] += 1
                o["ticket"] = cnt[o["eng"]]
        waited = {e: {} for e in self.eng}
        nwaits = 0
        for o in ops:
            e = o["eng"]
            eobj = self.eng[e]
            need = {}
            for d in o["deps"]:
                od = ops[d]
                if od["kind"] == "c":
                    if od["eng"] == e and e == "pe":
                        continue
                    k = "E" + od["eng"]
                else:
                    k = "D" + od["key"]
                need[k] = max(need.get(k, 0), od["ticket"])
            for k, v in need.items():
                if waited[e].get(k, 0) >= v:
                    continue
                eobj.wait_ge(self._sem(k), v)
                waited[e][k] = v
                nwaits += 1
            if o["kind"] == "c":
                inst = o["fn"](eobj)
                if o["signal"]:
                    inst.then_inc(self._sem("E" + e), 1)
            else:
                if o["nc_ok"]:
                    with self.nc.allow_non_contiguous_dma(reason="tiny one-time parameter layout"):
                        inst = eobj.dma_start(out=o["out"], in_=o["in"])
                else:
                    inst = eobj.dma_start(out=o["out"], in_=o["in"])
                inst.then_inc(self._sem("D" + o["key"]), 16)
        sp = self.eng["sp"]
        for key, v in self.dma_cnt.items():
            if waited["sp"].get("D" + key, 0) < v:
                sp.wait_ge(self._sem("D" + key), v)
        for e in self.COMPUTE:
            if cnt[e] and waited["sp"].get("E" + e, 0) < cnt[e]:
                sp.wait_ge(self._sem("E" + e), cnt[e])
        return len(ops), nwaits


class Arena:
    def __init__(self, tensor, nbytes):
        self.t = tensor
        self.n = nbytes
        self.top = 0
        self.peak = 0

    def alloc(self, cols, dt, parts=P):
        esz = 4 if dt == F32 else 2
        nb = (cols * esz + 31) // 32 * 32
        off = self.top
        self.top += nb
        self.peak = max(self.peak, self.top)
        assert self.top <= self.n, f"SBUF arena overflow: {self.top} > {self.n}"
        v = self.t[0:parts, off // 4:(off + nb) // 4]
        if dt != F32:
            v = v.bitcast(dt)
        return v[:, 0:cols]


def build_program(cfg):
    nc = bass.Bass("TRN2", target_bir_lowering=False)
    NT, NTOK, NPRE, NMAIN = cfg.nt, cfg.ntok, cfg.npre, cfg.nmain
    FF, FSB = cfg.ff, cfg.fsb
    NS = 16

    def din(name, shape):
        return nc.dram_tensor(name, list(shape), F32, kind="ExternalInput").ap()

    def dout(name, shape):
        return nc.dram_tensor(name, list(shape), F32, kind="ExternalOutput").ap()

    xpre = din("xpre", [NPRE * P, D])
    xmain = din("xmain", [NTOK, D])
    sgla = din("sgla", [NS, H, DK, DV])
    spool = din("spool", [NS, 15, 1024])
    norm_mix_g = din("norm_mix_g", [D])
    NBLK_IN = 11 * H
    w_in_blk = din("w_in_blk", [NBLK_IN, P, KC, WB])
    w_alr = din("w_alr", [P, KC, 16])
    w_alpha = din("w_alpha_up", [16, 1024])
    b_alpha = din("b_alpha", [1024])
    gla_g = din("gla_norm_g", [DV])
    pool_w = din("pool_w", [4, 256, 512])
    pool_scale = din("pool_scale", [D])
    w_out_blk = din("w_out_blk", [D // WB, P, KC, WB])
    norm_mlp_g = din("norm_mlp_g", [D])
    w_up_blk = din("w_up_blk", [FF // WB, P, KC, WB])
    w_down_blk = din("w_down_blk", [(FF // FSB) * (D // WB), P, FSB // P, WB])
    norm_final_g = din("norm_final_g", [D])
    cmat = din("cmat", [NCONST, P, P])
    selr_d = din("selr", [P, 16])

    y = dout("y", [NTOK, D])
    o_sgp = dout("o_sgp", [H, DK, DV])
    o_plp = dout("o_plp", [15, 1024])
    o_sgs = dout("o_sgs", [NS, H, DK, DV])
    o_pls = dout("o_pls", [NS, 15, 1024])

    es = contextlib.ExitStack()
    with es:
        ARENA_BYTES = 212736
        arena_t = es.enter_context(nc.sbuf_tensor("arena", [P, ARENA_BYTES // 4], F32))
        A = Arena(arena_t, ARENA_BYTES)
        ps = [es.enter_context(nc.psum_tensor(f"ps{i}", [P, 512], F32))[:, :] for i in range(8)]
        pg = Prog(nc, es)

        def psb(i, lo, n):
            return ps[i][:, lo:lo + n // 2].bitcast(BF16)

        cm = A.alloc(NCONST * P, BF16).rearrange("p (n c) -> p n c", n=NCONST)
        identF = A.alloc(P, F32)
        onesF = A.alloc(P, F32)
        ones_b = A.alloc(8, BF16)
        walpha = A.alloc(1024, BF16)
        glag = A.alloc(DV, F32)
        selr = A.alloc(16, F32)

        pg.dma("pool", cm, cmat.rearrange("n p c -> p n c"), "cm", writes=["cm"])
        pg.dma("sp", identF, cmat[C_ID], "idf", writes=["identF"])
        pg.dma("pool", walpha[0:16, :], w_alpha, "wa", writes=["walpha"])
        pg.dma("pool", walpha[16:17, :], b_alpha.rearrange("(o n) -> o n", o=1), "wa", writes=["walpha"])
        pg.dma("sp", glag, gla_g.partition_broadcast(P), "glag", writes=["glag"])
        pg.dma("sp", selr, selr_d, "selr", writes=["selr"])
        pg.op("dve", lambda e: e.memset(onesF, 1.0), writes=["onesF"])
        pg.op("dve", lambda e: e.memset(ones_b, 1.0), writes=["ones_b"])

        def CM(i):
            return cm[:, i, :]

        MT_COLS = max(KC * NTOK, 4 * D, (FSB // P) * NTOK)
        mT_flat = A.alloc(MT_COLS, BF16)
        mT = mT_flat[:, 0:KC * NTOK].rearrange("p (k t) -> p k t", k=KC)
        mark_B = A.top

        def rstd_chain(ss, tmp, rstd, n, tag, nparts=P):
            pg.op("dve", lambda e: e.tensor_scalar(out=tmp, in0=ss, scalar1=1.0 / n, scalar2=EPS,
                                                  op0=ALU.mult, op1=ALU.add), reads=[tag + "ss"], writes=[tag + "tmp"])
            pg.op("act", lambda e: e.activation(out=tmp, in_=tmp, func=AF.Ln), reads=[tag + "tmp"], writes=[tag + "tmp"])
            pg.op("act", lambda e: e.activation(out=rstd, in_=tmp, func=AF.Exp, scale=-0.5),
                  reads=[tag + "tmp"], writes=[tag + "rstd"])

        def phase_A_chunks(xsrc, tiles, slot_of, gb_mix, xt, xn, pfx, after=None):
            gtok = pfx + "gbmix"

            def s0():
                pg.dma("sp", gb_mix, norm_mix_g.partition_broadcast(P), pfx + "gbm", writes=[gtok])

            def s1(n, ti):
                b = n % 2
                bx = n % len(xn)
                xtb, xnb = xt[b], xn[bx]
                pg.dma("sp", xtb, xsrc[ti * P:(ti + 1) * P, :], f"{pfx}xt{b}", writes=[f"{pfx}xt{b}"])
                ss, tmp, rstd = stat[:, 8 * b:8 * b + 1], stat[:, 8 * b + 1:8 * b + 2], stat[:, 8 * b + 2:8 * b + 3]
                tg = f"A{b}"
                if n < 2:
                    pg.op("dve", lambda e: e.memset(ss, 0.0), writes=[tg + "ss"])
                pg.op("act", lambda e: e.activation(out=xnb, in_=xtb, func=AF.Square, accum_out=ss),
                      reads=[f"{pfx}xt{b}"], writes=[tg + "ss", f"{pfx}xn{bx}"])
                rstd_chain(ss, tmp, rstd, D, tg)
                pg.op("dve", lambda e: e.memset(ss, 0.0), reads=[tg + "tmp"], writes=[tg + "ss"])
                pg.op("dve", lambda e: e.scalar_tensor_tensor(out=xnb, in0=xtb, scalar=rstd, in1=gb_mix, op0=ALU.mult, op1=ALU.mult),
                      reads=[f"{pfx}xt{b}", tg + "rstd", gtok], writes=[f"{pfx}xn{bx}"])

            def s2(n, ti):
                bx = n % len(xn)
                xnb = xn[bx]
                sl = slot_of(ti)
                for g4 in range(4):
                    pb = g4 % 2
                    pt = psb(pb, 0, 512).rearrange("p (k c) -> p k c", k=4)

                    def tr(e, g4=g4, pt=pt):
                        inst = None
                        for j in range(4):
                            kc = g4 * 4 + j
                            inst = e.transpose(pt[:, j, :], xnb[:, kc * P:(kc + 1) * P], CM(C_ID))
                        return inst
                    pg.op("pe", tr, reads=[f"{pfx}xn{bx}", "cm"], writes=[f"ps{pb}"])
                    dst = hTs(sl)[:, g4 * 4:(g4 + 1) * 4, :]
                    if g4 % 2 == 0:
                        pg.op("act", lambda e, dst=dst, pt=pt: e.copy(out=dst, in_=pt), reads=[f"ps{pb}"], writes=[f"hT{sl}"])
                    else:
                        pg.op("dve", lambda e, dst=dst, pt=pt: e.tensor_copy(out=dst, in_=pt), reads=[f"ps{pb}"], writes=[f"hT{sl}"])

            tl = list(tiles)

            def wrap(fn):
                def g():
                    prev = pg.cur_after
                    pg.cur_after = after
                    try:
                        fn()
                    finally:
                        pg.cur_after = prev
                return g
            out = []
            if tl:
                out.append(wrap(lambda: (s0(), s1(0, tl[0]))))
            for n, ti in enumerate(tl):
                def ch(n=n, ti=ti):
                    if len(xn) > 1 and n + 1 < len(tl):
                        s1(n + 1, tl[n + 1])
                    s2(n, ti)
                    if len(xn) == 1 and n + 1 < len(tl):
                        s1(n + 1, tl[n + 1])
                out.append(wrap(ch))
            return out

        def phase_A(xsrc, tiles, slot_of):
            for ch in phase_A_chunks(xsrc, tiles, slot_of, gbmix, xt, xn, "P"):
                ch()

        wctr = [0]

        def load_w(dst_view, src_ap):
            s = wctr[0] % 2
            wctr[0] += 1
            pg.dma("pool", dst_view(s), src_ap, f"w{s}", writes=[f"w{s}"])
            return s

        zctr = [0]

        def dense_block(hT, slots, wsl, wview, ncols, dst_fn, ztoks, extra_fp32=None):
            for n, sl in enumerate(slots):
                pb = zctr[0] % 2
                zctr[0] += 1
                pz = ps[pb][:, 0:ncols]

                def mm(e, sl=sl, pz=pz):
                    inst = None
                    for kc in range(KC):
                        inst = e.matmul(pz, lhsT=hTs(sl)[:, kc, :], rhs=wview[:, kc, 0:ncols],
                                        start=(kc == 0), stop=(kc == KC - 1))
                    return inst
                pg.op("pe", mm, reads=[f"hT{sl}", f"w{wsl}"], writes=[f"ps{pb}"])
                dst = dst_fn(n)
                ex = extra_fp32(n) if extra_fp32 is not None else None
                use_act = ((zctr[0] % 2) == 0) or (ex is not None)
                if dst is not None:
                    if use_act:
                        pg.op("act", lambda e, dst=dst, pz=pz: e.copy(out=dst, in_=pz), reads=[f"ps{pb}"], writes=[ztoks[n]])
                    else:
                        pg.op("dve", lambda e, dst=dst, pz=pz: e.tensor_copy(out=dst, in_=pz), reads=[f"ps{pb}"], writes=[ztoks[n]])
                if ex is not None:
                    exdst, extok = ex
                    pg.op("act", lambda e, exdst=exdst, pz=pz: e.copy(out=exdst, in_=pz), reads=[f"ps{pb}"], writes=[extok])

        NSLOT_H = cfg.ntmax + 1
        hT = A.alloc(KC * NSLOT_H * P, BF16).rearrange("p (k t) -> p k t", k=KC)
        zbuf = [A.alloc(cfg.ntmax * HCOLS, BF16).rearrange("p (n c) -> p n c", n=cfg.ntmax) for _ in range(2)]
        wbuf = [A.alloc(KC * WB, BF16).rearrange("p (k c) -> p k c", k=KC) for _ in range(2)]
        _zb1 = zbuf[1].rearrange("p n c -> p (n c)")
        _zcols = cfg.ntmax * HCOLS
        N_XS = max(0, min(NPRE - NSLOT_H, (_zcols - NPRE * 768) // (KC * P)))
        hTx = [_zb1[:, _zcols - (i + 1) * KC * P:_zcols - i * KC * P].rearrange("p (k t) -> p k t", k=KC) for i in range(N_XS)]

        def hTs(sl):
            if sl < NSLOT_H:
                return hT[:, :, sl * P:(sl + 1) * P]
            return hTx[sl - NSLOT_H]
        S32 = A.alloc(2 * DV, F32).rearrange("p (c v) -> p c v", c=2)
        t_Sx = [A.alloc(DV, F32) for _ in range(4)]
        Sbf = A.alloc(2 * DV, BF16).rearrange("p (c v) -> p c v", c=2)
        ucarry = A.alloc(H * 256, BF16).rearrange("p (h c) -> p h c", h=H)
        alrT = A.alloc(max(NSLOT_H, NPRE) * P, BF16)
        poolw = A.alloc(2 * DV, BF16).rearrange("p (c v) -> p c v", c=2)
        pscale = A.alloc(DV, F32)
        bufS = A.alloc(2 * 1024, BF16).rearrange("p (f c) -> p f c", f=2)
        stat = A.alloc(16, F32)
        mark_tmp = A.top
        gbmix = A.alloc(D, F32)
        xt = [A.alloc(D, F32) for _ in range(2)]
        xn = [A.alloc(D, BF16) for _ in range(2)]
        A.top = mark_tmp
        t_eA = A.alloc(256, F32)
        t_sp = A.alloc(256, BF16)
        t_Ek = t_eA
        t_Eq = A.alloc(256, F32)
        t_ebl = A.alloc(32, F32)
        t_ke = A.alloc(256, BF16)
        t_qe = A.alloc(256, BF16)
        t_qkT = A.alloc(512, BF16).rearrange("p (k c) -> p k c", k=4)
        t_scT = A.alloc(P, BF16)
        t_f1 = A.alloc(DV, F32)
        t_f2 = A.alloc(DV, F32)
        t_f3 = A.alloc(DV, F32)
        t_m = A.alloc(DV, BF16)
        t_pT = A.alloc(256, BF16).rearrange("p (k c) -> p k c", k=2)
        t_tS = A.alloc(2 * DV, F32).rearrange("p (c v) -> p c v", c=2)
        t_junk = A.alloc(DV, BF16)
        t_u32 = A.alloc(256, F32)
        t_u32s = t_u32
        t_ma = A.alloc(DV, F32)
        t_Sj = A.alloc(2 * DV, F32).rearrange("p (c v) -> p c v", c=2)
        t_Sjb = A.alloc(8 * DV, BF16).rearrange("p (c v) -> p c v", c=8)
        t_qm = A.alloc(256, BF16).rearrange("p (k c) -> p k c", k=2)
        t_km = A.alloc(256, BF16)
        assert A.top >= mark_tmp + (3 * D * 4 + 2 * D * 2) or True

        _zb0 = zbuf[0].rearrange("p n c -> p (n c)")
        CAN_OVL = (cfg.ntmax * HCOLS >= 6 * D) and (A.n - A.top >= 2 * D + 64)
        if CAN_OVL:
            xtB = [_zb0[:, 0:2 * D].bitcast(F32), _zb0[:, 2 * D:4 * D].bitcast(F32)]
            gbmixB = _zb0[:, 4 * D:6 * D].bitcast(F32)
            xnB = A.alloc(D, BF16)
        pg.op("dve", lambda e: e.memset(alrT[0:32, :], 1.0), writes=["alrT"])
        pg.dma("pool", bufS[0:120, 0, :], spool[0:8].rearrange("s r c -> (s r) c"), "bufS", writes=["bufS"])
        pg.dma("pool", bufS[0:120, 1, :], spool[8:16].rearrange("s r c -> (s r) c"), "bufS", writes=["bufS"])
        pg.dma("sp", o_pls[:, 0:7, :], spool[:, 8:15, :], "opl0")

        def alr_block(slots):
            s = load_w(lambda s: wbuf[s][:, :, 0:16], w_alr)
            wv = wbuf[s]
            for n, sl in enumerate(slots):
                pa = ps[2][0:16, 0:P]

                def mm(e, sl=sl, pa=pa):
                    inst = None
                    for kc in range(KC):
                        inst = e.matmul(pa, lhsT=wv[:, kc, 0:16], rhs=hTs(sl)[:, kc, :],
                                        start=(kc == 0), stop=(kc == KC - 1))
                    return inst
                pg.op("pe", mm, reads=[f"hT{sl}", f"w{s}"], writes=["ps2"])
                dst = alrT[0:16, n * P:(n + 1) * P]
                pg.op("act", lambda e, dst=dst, pa=pa: e.copy(out=dst, in_=pa), reads=["ps2"], writes=["alrT"])

        def gla_step(h, V, ztok, n, kind, full, uprev=None, uprev_tok=None, band_first=False, tile_glob=None, part="all"):
            cum = CM(C_CUMP if kind == "P" else C_CUMS)
            mask = CM(C_MASKP if kind == "P" else C_MASKS)
            NL = 1 if kind == "P" else NS
            cumlast = cum[:, P - 1:P] if kind == "P" else cum.rearrange("p (j i) -> p j i", i=8)[:, :, 7]
            k, v = V["k"], V["v"]
            po = ps[4]
            pP = [ps[5], ps[6]]
            if full:
                r, ga, gb, u = V["r"], V["ga"], V["gb"], V["u"]
                q = V["q"]
            if part != "back":
                pa = ps[2][:, 0:256]
                pg.op("pe", lambda e: e.matmul(pa, lhsT=alrT[0:17, n * P:(n + 1) * P], rhs=walpha[0:17, h * 256:(h + 1) * 256],
                                               start=True, stop=True), reads=["alrT", "walpha"], writes=["ps2"])
                pg.op("act", lambda e: e.activation(out=t_eA, in_=pa, func=AF.Exp, scale=-1.0), reads=["ps2"], writes=["eA", "Ek"])
                pg.op("act", lambda e: e.activation(out=t_sp, in_=t_eA, func=AF.Ln, bias=1.0), reads=["eA"], writes=["sp"])
                if full:
                    r, ga, gb, u = V["r"], V["ga"], V["gb"], V["u"]
                    pg.op("act", lambda e: e.activation(out=t_f1, in_=r, func=AF.Exp, scale=-1.0), reads=[ztok], writes=["f1"])
                    pg.op("act", lambda e: e.activation(out=t_f2, in_=ga, func=AF.Exp, scale=-1.0), reads=[ztok], writes=["f2"])
                    pg.op("act", lambda e: e.activation(out=t_f1, in_=t_f1, func=AF.Ln, bias=1.0), reads=["f1"], writes=["f1"])
                    pg.op("act", lambda e: e.activation(out=t_f2, in_=t_f2, func=AF.Ln, bias=1.0), reads=["f2"], writes=["f2"])
                yield
                pb_ = ps[2][:, 256:512]
                pg.op("pe", lambda e: e.matmul(pb_, lhsT=cum, rhs=t_sp, start=True, stop=True), reads=["sp", "cm"], writes=["ps2"])
                pe_ = ps[3][:, 384:384 + 2 * NL]

                def mm_bl(e):
                    inst = None
                    for dc in range(2):
                        inst = e.matmul(pe_[:, dc * NL:(dc + 1) * NL], lhsT=t_sp[:, dc * P:(dc + 1) * P], rhs=cumlast,
                                        start=True, stop=True)
                    return inst
                pg.op("pe", mm_bl, reads=["sp", "cm"], writes=["ps3"])
                pg.op("act", lambda e: e.activation(out=t_Ek, in_=pb_, func=AF.Exp, scale=-1.0), reads=["ps2", "sp"], writes=["Ek", "eA"])
                if full:
                    pg.op("act", lambda e: e.activation(out=t_Eq, in_=pb_, func=AF.Exp, bias=LN_QSCALE), reads=["ps2"], writes=["Eq"])
                pg.op("act", lambda e: e.activation(out=t_ebl[:, 0:2 * NL], in_=pe_, func=AF.Exp), reads=["ps3"], writes=["ebl"])
                pg.op("dve", lambda e: e.tensor_tensor(out=t_ke, in0=k, in1=t_Ek, op=ALU.mult), reads=[ztok, "Ek"], writes=["ke"])
                if full:
                    q = V["q"]
                    pg.op("dve", lambda e: e.tensor_tensor(out=t_qe, in0=q, in1=t_Eq, op=ALU.mult), reads=[ztok, "Eq"], writes=["qe"])
                    pg.op("dve", lambda e: e.tensor_tensor(out=t_f1, in0=t_f1, in1=t_f2, op=ALU.add), reads=["f1", "f2"], writes=["f1"])
                    pg.op("act", lambda e: e.activation(out=t_f1, in_=t_f1, func=AF.Exp, scale=-1.0), reads=["f1"], writes=["f1"])
                    pg.op("dve", lambda e: e.tensor_tensor(out=t_f2, in0=r, in1=t_f1, op=ALU.mult), reads=[ztok, "f1"], writes=["f2"])
                    pg.op("dve", lambda e: e.tensor_tensor(out=t_f2, in0=t_f2, in1=glag, op=ALU.mult), reads=["f2", "glag"], writes=["f2"])
                yield
                if full:
                    ptq = psb(3, 0, 512).rearrange("p (k c) -> p k c", k=4)

                    def trq(e):
                        e.transpose(ptq[:, 0, :], t_qe[:, 0:P], CM(C_ID))
                        e.transpose(ptq[:, 1, :], t_qe[:, P:2 * P], CM(C_ID))
                        e.transpose(ptq[:, 2, :], t_ke[:, 0:P], CM(C_ID))
                        return e.transpose(ptq[:, 3, :], t_ke[:, P:2 * P], CM(C_ID))
                    pg.op("pe", trq, reads=["qe", "ke", "cm"], writes=["ps3"])
                    pg.op("act", lambda e: e.copy(out=t_qkT, in_=ptq), reads=["ps3"], writes=["qkT"])
                    yield
                    pss = ps[3][:, 256:384]

                    def mms(e):
                        e.matmul(pss, lhsT=t_qkT[:, 2, :], rhs=t_qkT[:, 0, :], start=True, stop=False)
                        return e.matmul(pss, lhsT=t_qkT[:, 3, :], rhs=t_qkT[:, 1, :], start=False, stop=True)
                    pg.op("pe", mms, reads=["qkT"], writes=["ps3"])
                    pg.op("dve", lambda e: e.tensor_tensor(out=t_scT, in0=pss, in1=mask, op=ALU.mult), reads=["ps3", "cm"], writes=["scT"])
                    yield
            if part == "front":
                return
            if kind == "P":
                if full:
                    def mmo(e):
                        e.matmul(po, lhsT=t_scT, rhs=v, start=True, stop=False)
                        e.matmul(po, lhsT=t_qkT[:, 0, :], rhs=Sbf[:, 0, :], start=False, stop=False)
                        return e.matmul(po, lhsT=t_qkT[:, 1, :], rhs=Sbf[:, 1, :], start=False, stop=True)
                    pg.op("pe", mmo, reads=["scT", "qkT", ztok, "Sbf"], writes=["ps4"])
                def mmp(e):
                    e.matmul(pP[0], lhsT=t_ke[:, 0:P], rhs=v, start=True, stop=True)
                    return e.matmul(pP[1], lhsT=t_ke[:, P:2 * P], rhs=v, start=True, stop=True)
                pg.op("pe", mmp, reads=["ke", ztok], writes=["ps5", "ps6"])
                for dc in range(2):
                    pg.op("dve", lambda e, dc=dc: e.tensor_tensor(out=t_tS[:, dc, :], in0=S32[:, dc, :], in1=pP[dc], op=ALU.add),
                          reads=["S32", f"ps{5 + dc}"], writes=[f"tS{dc}"])
                    pg.op("dve", lambda e, dc=dc: e.tensor_scalar(out=S32[:, dc, :], in0=t_tS[:, dc, :], scalar1=t_ebl[:, dc:dc + 1],
                                                                   scalar2=None, op0=ALU.mult),
                          reads=[f"tS{dc}", "ebl"], writes=["S32"])
                    pg.op("act", lambda e, dc=dc: e.mul(out=Sbf[:, dc, :], in_=t_tS[:, dc, :], mul=t_ebl[:, dc:dc + 1]),
                          reads=[f"tS{dc}", "ebl"], writes=["Sbf"])
            else:
                pg.op("pe", lambda e: e.matmul(po, lhsT=t_scT, rhs=v, start=True, stop=False),
                      reads=["scT", ztok], writes=["ps4"])
                prv = ps[2][:, 0:256]
                pg.op("pe", lambda e: e.matmul(prv, lhsT=CM(C_REVS), rhs=t_sp, start=True, stop=True), reads=["sp", "cm"], writes=["ps2"])
                pg.op("act", lambda e: e.activation(out=t_Eq, in_=prv, func=AF.Exp), reads=["ps2"], writes=["Eq"])
                pg.op("dve", lambda e: e.tensor_tensor(out=t_qe, in0=k, in1=t_Eq, op=ALU.mult), reads=[ztok, "Eq"], writes=["qe"])
                items = [(j, dc) for j in range(NS) for dc in range(2)]
                NB = 8
                fbuf = [t_Sj[:, 0, :], t_Sj[:, 1, :], t_tS[:, 0, :], t_tS[:, 1, :]] + t_Sx
                ftok = ["Sj0", "Sj1", "tS0", "tS1", "Sx0", "Sx1", "Sx2", "Sx3"]
                qmb = [t_qm, t_junk[:, 0:256].rearrange("p (k c) -> p k c", k=2)]
                kmb = [t_km, t_junk[:, 256:512]]

                def s_loads(i):
                    j, dc = items[i]
                    bb = i % NB
                    pg.dma("sp", fbuf[bb], sgla[j, h, dc * P:(dc + 1) * P, :], f"SL{bb}", writes=[ftok[bb]])

                def s_cast(i):
                    bb = i % NB
                    pg.op("act", lambda e: e.copy(out=t_Sjb[:, bb, :], in_=fbuf[bb]), reads=[ftok[bb]], writes=[f"Sjb{bb}"])

                def s_prep(j):
                    q2, k2, jb = qmb[j % 2], kmb[j % 2], j % 2
                    pg.op("dve", lambda e: e.memset(q2.rearrange("p k c -> p (k c)"), 0.0), writes=[f"qm{jb}"])
                    pg.op("dve", lambda e: e.tensor_copy(out=q2[:, :, 8 * j:8 * j + 8], in_=t_qkT[:, 0:2, 8 * j:8 * j + 8]),
                          reads=["qkT"], writes=[f"qm{jb}"])
                    pg.op("dve", lambda e: e.tensor_scalar(out=k2, in0=t_qe, scalar1=selr[:, j:j + 1], scalar2=None, op0=ALU.mult),
                          reads=["qe", "selr"], writes=[f"km{jb}"])

                def s_tail(i):
                    j, dc = items[i]
                    bb, pb2 = i % NB, i % 2
                    col = dc * NS + j
                    pg.op("dve", lambda e: e.scalar_tensor_tensor(out=fbuf[bb], in0=fbuf[bb], scalar=t_ebl[:, col:col + 1], in1=pP[pb2],
                                                                  op0=ALU.mult, op1=ALU.add),
                          reads=[ftok[bb], f"ps{5 + pb2}", "ebl"], writes=[ftok[bb]])
                    pg.dma("act", o_sgs[j, h, dc * P:(dc + 1) * P, :], fbuf[bb], f"SS{bb}", reads=[ftok[bb]])

                for i0 in range(min(NB - 1, len(items))):
                    s_loads(i0)
                s_prep(0)
                s_cast(0)
                for i, (j, dc) in enumerate(items):
                    bb, pb2, jb = i % NB, i % 2, j % 2
                    if dc == 0 and j + 1 < NS:
                        s_prep(j + 1)
                    if i + 1 < len(items):
                        s_cast(i + 1)
                    last = (i == len(items) - 1)
                    pg.op("pe", lambda e, dc=dc, bb=bb, last=last, jb=jb: e.matmul(po, lhsT=qmb[jb][:, dc, :], rhs=t_Sjb[:, bb, :], start=False, stop=last),
                          reads=[f"qm{jb}", f"Sjb{bb}"], writes=["ps4"])
                    pg.op("pe", lambda e, dc=dc, pb2=pb2, jb=jb: e.matmul(pP[pb2], lhsT=kmb[jb][:, dc * P:(dc + 1) * P], rhs=v, start=True, stop=True),
                          reads=[f"km{jb}", ztok], writes=[f"ps{5 + pb2}"])
                    if i >= 1:
                        s_tail(i - 1)
                    if i + NB - 1 < len(items):
                        s_loads(i + NB - 1)
                    if i % 4 == 3:
                        yield
                s_tail(len(items) - 1)
                yield
            if not full:
                yield
                return
            ss, tmp, rstd = stat[:, 4:5], stat[:, 5:6], stat[:, 6:7]
            pg.op("dve", lambda e: e.memset(ss, 0.0), writes=["Gss"])
            pg.op("act", lambda e: e.activation(out=t_f3, in_=po, func=AF.Square, accum_out=ss), reads=["ps4"], writes=["Gss", "f3"])
            rstd_chain(ss, tmp, rstd, DV, "G")
            pg.op("dve", lambda e: e.scalar_tensor_tensor(out=t_ma, in0=po, scalar=rstd, in1=t_f2, op0=ALU.mult, op1=ALU.mult),
                  reads=["ps4", "Grstd", "f2"], writes=["ma"])
            pg.op("act", lambda e: e.activation(out=t_f3, in_=gb, func=AF.Exp, scale=-1.0), reads=[ztok], writes=["f3"])
            pg.op("act", lambda e: e.activation(out=t_f3, in_=t_f3, func=AF.Ln, bias=1.0), reads=["f3"], writes=["f3"])
            pg.op("act", lambda e: e.activation(out=t_f3, in_=t_f3, func=AF.Exp, scale=-1.0), reads=["f3"], writes=["f3"])
            pg.op("dve", lambda e: e.tensor_tensor(out=t_f3, in0=t_f3, in1=pscale, op=ALU.mult), reads=["f3", "pscale"], writes=["f3"])
            yield
            g = h
            ppt = ps[7][:, 0:256]
            if kind == "P":
                bcur = CM((C_BFIRST if band_first else C_BCUR) + g)

                def mmpool(e):
                    inst = None
                    for cc in range(2):
                        e.matmul(ppt[:, cc * P:(cc + 1) * P], lhsT=uprev[:, cc * P:(cc + 1) * P], rhs=CM(C_BPREV + g), start=True, stop=False)
                        inst = e.matmul(ppt[:, cc * P:(cc + 1) * P], lhsT=u[:, cc * P:(cc + 1) * P], rhs=bcur, start=False, stop=True)
                    return inst
                pg.op("pe", mmpool, reads=[ztok, uprev_tok, "cm"], writes=["ps7"])
            else:
                def mmpool(e):
                    inst = None
                    for cc in range(2):
                        c0 = g * 256 + cc * P
                        e.matmul(ppt[:, cc * P:(cc + 1) * P], lhsT=bufS[0:120, 0, c0:c0 + P], rhs=cm[0:120, C_BSB + 2 * g, :], start=True, stop=False)
                        e.matmul(ppt[:, cc * P:(cc + 1) * P], lhsT=bufS[0:120, 1, c0:c0 + P], rhs=cm[0:120, C_BSB + 2 * g + 1, :], start=False, stop=False)
                        inst = e.matmul(ppt[:, cc * P:(cc + 1) * P], lhsT=u[:, cc * P:(cc + 1) * P], rhs=CM(C_BSC + g), start=False, stop=True)
                    return inst
                pg.op("pe", mmpool, reads=[ztok, "bufS", "cm"], writes=["ps7"])
            pg.op("act", lambda e: e.copy(out=t_pT.rearrange("p k c -> p (k c)"), in_=ppt), reads=["ps7"], writes=["pT"])
            yield
            def mmob(e):
                e.matmul(po, lhsT=t_pT[:, 0, :], rhs=poolw[:, 0, :], start=True, stop=False)
                return e.matmul(po, lhsT=t_pT[:, 1, :], rhs=poolw[:, 1, :], start=False, stop=True)
            pg.op("pe", mmob, reads=["pT", "poolw"], writes=["ps4"])
            pg.op("dve", lambda e: e.tensor_tensor(out=t_f3, in0=po, in1=t_f3, op=ALU.mult), reads=["ps4", "f3"], writes=["f3"])
            pg.op("dve", lambda e: e.tensor_tensor(out=t_m, in0=t_ma, in1=t_f3, op=ALU.add), reads=["ma", "f3"], writes=["m"])
            yield
            ptm = psb(7, 256, 512).rearrange("p (k c) -> p k c", k=4)

            def trm(e):
                inst = None
                for j in range(4):
                    inst = e.transpose(ptm[:, j, :], t_m[:, j * P:(j + 1) * P], CM(C_ID))
                return inst
            pg.op("pe", trm, reads=["m", "cm"], writes=["ps7"])
            dst = mT[:, h * 4:(h + 1) * 4, tile_glob * P:(tile_glob + 1) * P]
            pg.op("act", lambda e: e.copy(out=dst, in_=ptm), reads=["ps7"], writes=["mT"])
            yield

        def zip_steps(fronts, backs):
            if not fronts:
                return
            for _ in fronts[0]():
                yield
            for n in range(len(backs)):
                b = backs[n]()
                f = fronts[n + 1]() if n + 1 < len(fronts) else None
                while b is not None or f is not None:
                    if b is not None:
                        try:
                            next(b)
                            yield
                        except StopIteration:
                            b = None
                    if f is not None:
                        try:
                            next(f)
                            yield
                        except StopIteration:
                            f = None

        s_init = [False] * H

        def state_in(h):
            if not s_init[h]:
                s_init[h] = True
                pg.op("dve", lambda e: e.memset(S32.rearrange("p c v -> p (c v)"), 0.0), writes=["S32"])
            else:
                pg.dma("sp", S32, o_sgp[h].rearrange("(c p) v -> p c v", p=P), "dSl", reads=[f"dramS{h}"], writes=["S32"])

        def state_out(h):
            pg.dma("sp", o_sgp[h].rearrange("(c p) v -> p c v", p=P), S32, "dSs", reads=["S32"], writes=[f"dramS{h}"])

        def n_stages(kind, full):
            if not full:
                return 3
            return 8 + ((NS // 2 + 1) if kind == "S" else 0)

        def finish():
            nops, nwaits = pg.emit()
            print(f"[kernel] ops={nops} waits={nwaits} sbuf_peak={A.peak} sems={len(pg.sems)}")

        def wsrc(bid, ncols):
            return w_in_blk[bid]

        def interleave(chunks, chain, nst):
            left = nst
            for i, ch in enumerate(chunks):
                ch()
                if chain is not None and left > 0:
                    k = -(-left // (len(chunks) - i))
                    for _ in range(k):
                        try:
                            next(chain)
                        except StopIteration:
                            chain = None
                            left = 0
                            break
                        left -= 1
            if chain is not None:
                for _ in chain:
                    pass

        def dense_chunks(slots, colspecs, dst_of, ztoks, extra_of=None, after_block=None):
            out = []
            for bi, (c0, zoff) in enumerate(colspecs):
                state = {}
                for n, sl in enumerate(slots):
                    def ch(bi=bi, c0=c0, zoff=zoff, n=n, sl=sl, state=state):
                        if n == 0:
                            state["s"] = load_w(lambda s: wbuf[s], wsrc(c0, WB))
                        s = state["s"]
                        ex = (lambda _n, n=n: extra_of(bi, n)) if extra_of is not None else None
                        dense_block(hT, [sl], s, wbuf[s], WB, lambda _n, n=n, zoff=zoff: dst_of(n, zoff), [ztoks[n]], extra_fp32=ex)
                        if after_block is not None:
                            after_block(bi, n, s)
                    out.append(ch)
            return out

        d0_done = {}
        p_done_pending = {}
        blocks = [("q", ZQ, lambda h: h * 256), ("k", ZK, lambda h: 1024 + h * 256),
                  ("v0", ZV, lambda h: 2048 + h * 512), ("v1", ZV + 256, lambda h: 2048 + h * 512 + 256),
                  ("r0", ZR, lambda h: 4096 + h * 512), ("r1", ZR + 256, lambda h: 4096 + h * 512 + 256),
                  ("ga0", ZGA, lambda h: 7184 + h * 512), ("ga1", ZGA + 256, lambda h: 7184 + h * 512 + 256),
                  ("gb0", ZGB, lambda h: 9232 + h * 512), ("gb1", ZGB + 256, lambda h: 9232 + h * 512 + 256),
                  ("u", ZU, lambda h: 6160 + h * 256)]
        UBI = len(blocks) - 1
        a_done = {}
        PRE_LAST_SLOT = NSLOT_H - 1
        PG = min(NPRE, NSLOT_H + N_XS) if NPRE > 0 else 1
        if NPRE > PG:
            PG = min(4, cfg.ntmax)

        def _pre_slot(ti, g0):
            if ti == NPRE - 1:
                return PRE_LAST_SLOT
            r = ti - g0
            return r if r < NSLOT_H - 1 else r + 1
        def make_dense(sbi, h):
            tiles = cfg.sbs[sbi]
            nts = len(tiles)
            slots = list(range(nts))
            zs = h % 2
            zb = zbuf[zs]
            ztoks = [f"z{zs}_{n}" for n in range(nts)]
            specs = [(h * 11 + bi_, zoff) for bi_, (nm, zoff, cfn) in enumerate(blocks)]

            def extra_of(bi, n):
                if bi != UBI:
                    return None
                tg = tiles[n]
                if tg == NMAIN - 1:
                    return (t_u32, "u32")
                if tg == NT - 1:
                    return (t_u32s, "u32")
                return None

            def after_block(bi, n, s):
                if bi != UBI:
                    return
                tg = tiles[n]
                if tg == NMAIN - 1:
                    pg.dma("sp", o_plp[:, h * 256:(h + 1) * 256], t_u32[P - 15:P, :], "oplu", reads=["u32"])
                if tg == NT - 1:
                    for j in range(NS):
                        pg.dma("sp", o_pls[j, 7:15, h * 256:(h + 1) * 256], t_u32s[8 * j:8 * j + 8, :], "oplu", reads=["u32"])
                if n == len(tiles) - 1 and sbi == 0 and NPRE > 0:
                    dense_block(hT, [PRE_LAST_SLOT], s, wbuf[s], WB, lambda n_: ucarry[:, h, :], [f"uc{h}"])

            return dense_chunks(slots, specs, lambda n, zoff: zb[:, n, zoff:zoff + WB], ztoks,
                                extra_of=extra_of, after_block=after_block)

        def make_chain(sbi, h):
            tiles = cfg.sbs[sbi]
            nts = len(tiles)
            zs = h % 2
            zb = zbuf[zs]
            ztoks = [f"z{zs}_{n}" for n in range(nts)]

            def gen():
                pg.dma("pool", poolw, pool_w[h].rearrange("(c p) v -> p c v", p=P), "pw", writes=["poolw"])
                pg.dma("sp", pscale, pool_scale[h * 512:(h + 1) * 512].partition_broadcast(P), "psc", writes=["pscale"])
                state_in(h)
                pg.op("act", lambda e: e.copy(out=Sbf.rearrange("p c v -> p (c v)"), in_=S32.rearrange("p c v -> p (c v)")),
                      reads=["S32"], writes=["Sbf"])
                fr, bk = [], []
                last_prompt_n = None
                for n, tg in enumerate(tiles):
                    zt = zb[:, n, :]
                    V = {"q": zt[:, ZQ:ZQ + 256], "k": zt[:, ZK:ZK + 256], "v": zt[:, ZV:ZV + 512], "r": zt[:, ZR:ZR + 512],
                         "ga": zt[:, ZGA:ZGA + 512], "gb": zt[:, ZGB:ZGB + 512], "u": zt[:, ZU:ZU + 256]}
                    if tg < NMAIN:
                        if n == 0:
                            up, upt = ucarry[:, h, :], f"uc{h}"
                        else:
                            up, upt = zb[:, n - 1, ZU:ZU + 256], ztoks[n - 1]
                        kw = dict(uprev=up, uprev_tok=upt, band_first=(tg == 0), tile_glob=tg)
                        fr.append(lambda n=n, V=V, kw=kw: gla_step(h, V, ztoks[n], n, "P", True, part="front", **kw))
                        bk.append(lambda n=n, V=V, kw=kw: gla_step(h, V, ztoks[n], n, "P", True, part="back", **kw))
                        last_prompt_n = n
                    else:
                        fr.append(lambda n=n, V=V, tg=tg: gla_step(h, V, ztoks[n], n, "S", True, tile_glob=tg, part="front"))
                        bk.append(lambda n=n, V=V, tg=tg: gla_step(h, V, ztoks[n], n, "S", True, tile_glob=tg, part="back"))
                yield from zip_steps(fr, bk)
                state_out(h)
                if last_prompt_n is not None and sbi + 1 < len(cfg.sbs):
                    pg.op("act", lambda e, n=last_prompt_n: e.copy(out=ucarry[:, h, :], in_=zb[:, n, ZU:ZU + 256]),
                          reads=[ztoks[last_prompt_n]], writes=[f"uc{h}"])
            nst = sum(n_stages("P" if tg < NMAIN else "S", True) for tg in tiles)
            return gen(), nst

        def with_after(fn, tag):
            def g():
                prev = pg.cur_after
                pg.cur_after = tag
                try:
                    fn()
                finally:
                    pg.cur_after = prev
            return g

        for g0 in range(0, NPRE, PG):
            ptiles = list(range(g0, min(NPRE, g0 + PG)))
            slot_of = lambda ti, g0=g0: _pre_slot(ti, g0)
            pslots = [slot_of(ti) for ti in ptiles]
            npt = len(ptiles)
            phase_A(xpre, ptiles, slot_of)
            pg.fence()
            alr_block(pslots)
            chain, nst = None, 0
            for h in range(H):
                zs = h % 2
                zflat = zbuf[zs].rearrange("p n c -> p (n c)")
                zp = zflat[:, 0:npt * 768].rearrange("p (n c) -> p n c", n=npt)
                ztoks = [f"z{zs}_{n}" for n in range(npt)]
                specs = [(h * 11 + 1, 0), (h * 11 + 2, 256), (h * 11 + 3, 512)]
                chunks = dense_chunks(pslots, specs, lambda n, zoff, zp=zp: zp[:, n, zoff:zoff + WB], ztoks)
                interleave(chunks, chain, nst)

                def mk_chain(h=h, zp=zp, ztoks=ztoks, npt=npt):
                    state_in(h)
                    fr, bk = [], []
                    for n in range(npt):
                        Vn = {"k": zp[:, n, 0:256], "v": zp[:, n, 256:768]}
                        fr.append(lambda n=n, Vn=Vn: gla_step(h, Vn, ztoks[n], n, "P", False, part="front"))
                        bk.append(lambda n=n, Vn=Vn: gla_step(h, Vn, ztoks[n], n, "P", False, part="back"))
                    yield from zip_steps(fr, bk)
                    state_out(h)
                chain, nst = mk_chain(), 3 * npt
            if CAN_OVL and g0 + PG >= NPRE:
                pg.snapshot("tail")
                t0_ = cfg.sbs[0]
                chs = phase_A_chunks(xmain, t0_, lambda ti, t0=t0_[0]: ti - t0, gbmixB, xtB, [xnB], "Q", after="tail")
                chs.append(lambda: pg.snapshot("Adone"))
                chs += [with_after(c, "Adone") for c in make_dense(0, 0)]
                interleave(chs, chain, nst)
                pg.snapshot("P_done")
                a_done[0] = True
                d0_done[0] = True
                p_done_pending[0] = True
            else:
                interleave([], chain, nst)
                pg.fence()

        for sbi, tiles in enumerate(cfg.sbs):
            nts = len(tiles)
            slots = list(range(nts))
            if not a_done.get(sbi):
                phase_A(xmain, tiles, lambda ti, t0=tiles[0]: ti - t0)
                pg.fence()
            alr_block(slots)
            chain, nst = None, 0
            for h in range(H):
                if h == 0 and d0_done.get(sbi):
                    pass
                else:
                    dch = make_dense(sbi, h)
                    if p_done_pending.get(sbi):
                        dch = [with_after(c, "P_done") for c in dch]
                    interleave(dch, chain, nst)
                if sbi == 0 and NPRE == 0:
                    pg.op("dve", lambda e, h=h: e.memset(ucarry[:, h, :], 0.0), writes=[f"uc{h}"])
                chain, nst = make_chain(sbi, h)
            if CAN_OVL and sbi + 1 < len(cfg.sbs):
                pg.snapshot("tail")
                t1_ = cfg.sbs[sbi + 1]
                chs = phase_A_chunks(xmain, t1_, lambda ti, t0=t1_[0]: ti - t0, gbmixB, xtB, [xnB], "Q", after="tail")
                chs.append(lambda: pg.snapshot("Adone"))
                chs += [with_after(c, "Adone") for c in make_dense(sbi + 1, 0)]
                interleave(chs, chain, nst)
                a_done[sbi + 1] = True
                d0_done[sbi + 1] = True
            else:
                interleave([], chain, nst)
                pg.fence()

        A.top = mark_B
        acc = A.alloc(KC * NTOK, F32).rearrange("p (k t) -> p k t", k=KC)
        h2T = A.alloc(KC * NTOK, BF16).rearrange("p (k t) -> p k t", k=KC)
        wb2 = [A.alloc(KC * WB, BF16).rearrange("p (k c) -> p k c", k=KC) for _ in range(2)]
        sqb = [A.alloc(512, BF16) for _ in range(2)]
        rrow = A.alloc(NTOK, F32)
        rbc = A.alloc(NTOK, F32)
        gT = A.alloc(2 * KC, F32)
        xt2 = [A.alloc(D, F32) for _ in range(2)]
        NFC = FSB // P
        actT = mT_flat[:, 0:NFC * NTOK].rearrange("p (k t) -> p k t", k=NFC)
        relu_t = [A.alloc(512, BF16) for _ in range(2)]
        tbs = [(t0, min(512, NTOK - t0)) for t0 in range(0, NTOK, 512)]

        pg.dma("sp", gT[:, 0:KC], norm_mlp_g.rearrange("(k p) -> p k", p=P), "gT", writes=["gT"], nc_ok=True)
        pg.dma("sp", gT[:, KC:2 * KC], norm_final_g.rearrange("(k p) -> p k", p=P), "gT", writes=["gT"], nc_ok=True)

        w2ctr = [0]

        def load_w2(view_fn, src):
            s = w2ctr[0] % 2
            w2ctr[0] += 1
            pg.dma("pool", view_fn(wb2[s]), src, f"v{s}", writes=[f"v{s}"])
            return s

        pctr = [0]

        def next_ps():
            pctr[0] += 1
            return pctr[0] % 4

        pg.op("dve", lambda e: e.memset(acc.rearrange("p k t -> p (k t)"), 0.0), writes=[f"acc{kc}" for kc in range(KC)])
        c2_chunks = []
        for ob in range(D // WB):
            st_ = {}
            for occ in range(WB // P):
                oc = ob * (WB // P) + occ
                for (t0, tn) in tbs:
                    def c2(ob=ob, occ=occ, oc=oc, t0=t0, tn=tn, st_=st_):
                        if "s" not in st_:
                            st_["s"] = load_w2(lambda w: w, w_out_blk[ob])
                        s = st_["s"]
                        pb = next_ps()
                        pz = ps[pb][:, 0:tn]

                        def mm(e):
                            inst = None
                            for kc in range(KC):
                                inst = e.matmul(pz, lhsT=wb2[s][:, kc, occ * P:(occ + 1) * P], rhs=mT[:, kc, t0:t0 + tn],
                                                start=(kc == 0), stop=(kc == KC - 1))
                            return inst
                        pg.op("pe", mm, reads=[f"v{s}", "mT"], writes=[f"ps{pb}"])
                        dst = acc[:, oc, t0:t0 + tn]
                        pg.op("dve", lambda e: e.tensor_tensor(out=dst, in0=dst, in1=pz, op=ALU.add),
                              reads=[f"ps{pb}", f"acc{oc}"], writes=[f"acc{oc}"])
                    c2_chunks.append(c2)
        c1_units = []
        for ti in range(NT):
            for g4 in range(4):
                def c1(ti=ti, g4=g4):
                    b = ti % 2
                    if g4 == 0:
                        pg.dma("sp", xt2[b], xmain[ti * P:(ti + 1) * P, :], f"x2{b}", writes=[f"x2{b}"])
                    pb = 4 + (g4 % 2)
                    pt = ps[pb].rearrange("p (k c) -> p k c", k=4)

                    def tr(e):
                        inst = None
                        for j in range(4):
                            kc = g4 * 4 + j
                            inst = e.transpose(pt[:, j, :], xt2[b][:, kc * P:(kc + 1) * P], identF)
                        return inst
                    pg.op("pe", tr, reads=[f"x2{b}", "identF"], writes=[f"ps{pb}"])
                    dst = acc[:, g4 * 4:(g4 + 1) * 4, ti * P:(ti + 1) * P]
                    toks = [f"acc{g4 * 4 + j}" for j in range(4)]
                    pg.op("dve", lambda e: e.tensor_tensor(out=dst, in0=dst, in1=pt, op=ALU.add),
                          reads=[f"ps{pb}"] + toks, writes=toks)
                c1_units.append(c1)
        left = len(c1_units)
        ui = 0
        for i, ch in enumerate(c2_chunks):
            ch()
            k = -(-left // (len(c2_chunks) - i)) if left > 0 else 0
            for _ in range(k):
                c1_units[ui]()
                ui += 1
                left -= 1
        while ui < len(c1_units):
            c1_units[ui]()
            ui += 1

        def rms_rows(tag):
            acct = [f"acc{kc}" for kc in range(KC)]
            for (t0, tn) in tbs:
                psr = ps[6][0:1, 0:tn]
                for kc in range(KC):
                    sb_ = sqb[kc % 2]
                    if kc % 2 == 0:
                        pg.op("act", lambda e, kc=kc, sb_=sb_, t0=t0, tn=tn: e.activation(out=sb_[:, 0:tn], in_=acc[:, kc, t0:t0 + tn], func=AF.Square),
                              reads=[f"acc{kc}"], writes=[f"sq{kc % 2}"])
                    else:
                        pg.op("dve", lambda e, kc=kc, sb_=sb_, t0=t0, tn=tn: e.tensor_tensor(out=sb_[:, 0:tn], in0=acc[:, kc, t0:t0 + tn],
                                                                                          in1=acc[:, kc, t0:t0 + tn], op=ALU.mult),
                              reads=[f"acc{kc}"], writes=[f"sq{kc % 2}"])
                    pg.op("pe", lambda e, kc=kc, sb_=sb_, tn=tn, psr=psr: e.matmul(psr, lhsT=ones_b[:, 0:1], rhs=sb_[:, 0:tn],
                                                                                  start=(kc == 0), stop=(kc == KC - 1)),
                          reads=[f"sq{kc % 2}", "ones_b"], writes=["ps6"])
                rr = rrow[0:1, t0:t0 + tn]
                pg.op("dve", lambda e, rr=rr, psr=psr: e.tensor_scalar(out=rr, in0=psr, scalar1=1.0 / D, scalar2=EPS, op0=ALU.mult, op1=ALU.add),
                      reads=["ps6"], writes=["rrow"])
                pg.op("act", lambda e, rr=rr: e.activation(out=rr, in_=rr, func=AF.Ln), reads=["rrow"], writes=["rrow"])
                pg.op("act", lambda e, rr=rr: e.activation(out=rr, in_=rr, func=AF.Exp, scale=-0.5), reads=["rrow"], writes=["rrow"])
                pbc = ps[7][:, 0:tn]
                pg.op("pe", lambda e, rr=rr, pbc=pbc: e.matmul(pbc, lhsT=onesF[0:1, :], rhs=rr, start=True, stop=True),
                      reads=["rrow", "onesF"], writes=["ps7"])
                pg.op("act", lambda e, pbc=pbc, t0=t0, tn=tn: e.copy(out=rbc[:, t0:t0 + tn], in_=pbc), reads=["ps7"], writes=["rbc"])

        rms_rows("mlp")
        for kc in range(KC):
            pg.op("dve", lambda e, kc=kc: e.scalar_tensor_tensor(out=h2T[:, kc, :], in0=acc[:, kc, :], scalar=gT[:, kc:kc + 1], in1=rbc,
                                                                   op0=ALU.mult, op1=ALU.mult),
                  reads=[f"acc{kc}", "gT", "rbc"], writes=["h2T"])
        pg.fence()

        for f in range(FF // FSB):
            for ub in range(FSB // WB):
                c0 = f * FSB + ub * WB
                s = load_w2(lambda w: w, w_up_blk[c0 // WB])
                for fc in range(WB // P):
                    ffc = ub * (WB // P) + fc
                    for (t0, tn) in tbs:
                        pb = next_ps()
                        pz = ps[pb][:, 0:tn]

                        def mm(e, s=s, fc=fc, t0=t0, tn=tn, pz=pz):
                            inst = None
                            for kc in range(KC):
                                inst = e.matmul(pz, lhsT=wb2[s][:, kc, fc * P:(fc + 1) * P], rhs=h2T[:, kc, t0:t0 + tn],
                                                start=(kc == 0), stop=(kc == KC - 1))
                            return inst
                        pg.op("pe", mm, reads=[f"v{s}", "h2T"], writes=[f"ps{pb}"])
                        rt = relu_t[pb % 2]
                        pg.op("act", lambda e, rt=rt, pz=pz, tn=tn: e.activation(out=rt[:, 0:tn], in_=pz, func=AF.Relu),
                              reads=[f"ps{pb}"], writes=[f"relu{pb % 2}"])
                        dst = actT[:, ffc, t0:t0 + tn]
                        pg.op("dve", lambda e, rt=rt, dst=dst, tn=tn: e.tensor_tensor(out=dst, in0=rt[:, 0:tn], in1=rt[:, 0:tn], op=ALU.mult),
                              reads=[f"relu{pb % 2}"], writes=[f"actT{ffc}"])
            for ob in range(D // WB):
                s = load_w2(lambda w: w[:, 0:NFC, :], w_down_blk[f * (D // WB) + ob])
                for occ in range(WB // P):
                    oc = ob * (WB // P) + occ
                    for (t0, tn) in tbs:
                        pb = next_ps()
                        pz = ps[pb][:, 0:tn]

                        def mm(e, s=s, occ=occ, t0=t0, tn=tn, pz=pz):
                            inst = None
                            for k in range(NFC):
                                inst = e.matmul(pz, lhsT=wb2[s][:, k, occ * P:(occ + 1) * P], rhs=actT[:, k, t0:t0 + tn],
                                                start=(k == 0), stop=(k == NFC - 1))
                            return inst
                        pg.op("pe", mm, reads=[f"v{s}"] + [f"actT{k}" for k in range(NFC)], writes=[f"ps{pb}"])
                        dst = acc[:, oc, t0:t0 + tn]
                        pg.op("dve", lambda e, dst=dst, pz=pz: e.tensor_tensor(out=dst, in0=dst, in1=pz, op=ALU.add),
                              reads=[f"ps{pb}", f"acc{oc}"], writes=[f"acc{oc}"])
        pg.fence()

        rms_rows("fin")
        for kc in range(KC):
            pg.op("dve", lambda e, kc=kc: e.scalar_tensor_tensor(out=acc[:, kc, :], in0=acc[:, kc, :], scalar=gT[:, KC + kc:KC + kc + 1], in1=rbc,
                                                                   op0=ALU.mult, op1=ALU.mult),
                  reads=[f"acc{kc}", "gT", "rbc"], writes=[f"acc{kc}"])
        for ti in range(NT):
            b = ti % 2
            yt = xt2[b]
            for g4 in range(4):
                pb = g4 % 4
                pt = ps[pb].rearrange("p (k c) -> p k c", k=4)

                def tr(e, g4=g4, pt=pt, ti=ti):
                    inst = None
                    for j in range(4):
                        kc = g4 * 4 + j
                        inst = e.transpose(pt[:, j, :], acc[:, kc, ti * P:(ti + 1) * P], identF)
                    return inst
                pg.op("pe", tr, reads=[f"acc{g4 * 4 + j}" for j in range(4)] + ["identF"], writes=[f"ps{pb}"])
                dst = yt[:, g4 * 512:(g4 + 1) * 512]
                if g4 % 2 == 0:
                    pg.op("act", lambda e, dst=dst, pb=pb: e.copy(out=dst, in_=ps[pb]), reads=[f"ps{pb}"], writes=[f"x2{b}"])
                else:
                    pg.op("dve", lambda e, dst=dst, pb=pb: e.tensor_copy(out=dst, in_=ps[pb]), reads=[f"ps{pb}"], writes=[f"x2{b}"])
            pg.dma("sp", y[ti * P:(ti + 1) * P, :], yt, f"y{b}", reads=[f"x2{b}"])

        finish()
    return nc


def make_consts(first_half):
    c = np.zeros((NCONST, P, P), np.float32)
    idx = np.arange(P)
    s_, t_ = idx[:, None], idx[None, :]
    c[C_ID] = np.eye(P)
    tri = (s_ <= t_).astype(np.float32)
    same = ((s_ // 8) == (t_ // 8)).astype(np.float32)
    c[C_MASKP] = tri
    c[C_CUMP] = -tri / 16.0
    c[C_MASKS] = tri * same
    c[C_CUMS] = -tri * same / 16.0
    for g, w in enumerate((2, 4, 8, 16)):
        eye = np.eye(P, dtype=np.float32)
        cur = ((s_ <= t_) & (s_ > t_ - w)).astype(np.float32) / w - eye
        c[C_BCUR + g] = cur
        c[C_BPREV + g] = ((s_ - P) > (t_ - w)).astype(np.float32) / w
        if first_half:
            cnt = np.minimum(t_ + 1, w).astype(np.float32)
            c[C_BFIRST + g] = ((s_ <= t_) & (s_ > t_ - w)).astype(np.float32) / cnt - eye
        else:
            c[C_BFIRST + g] = cur
        i_s, i_t = s_ % 8, t_ % 8
        c[C_BSC + g] = (same * ((i_s <= i_t) & (i_s > i_t - w))).astype(np.float32) / w - eye
        for hf in range(2):
            rows = np.arange(120)
            jr = rows // 15 + 8 * hf
            r = rows % 15
            m = (jr[:, None] == (idx[None, :] // 8)) & ((r[:, None] - 15) > ((idx[None, :] % 8) - w))
            c[C_BSB + 2 * g + hf, 0:120, :] = m.astype(np.float32) / w
    c[C_REVS] = -(same * (s_ > t_)).astype(np.float32) / 16.0
    selr = ((idx[:, None] // 8) == np.arange(16)[None, :]).astype(np.float32)
    return c, selr


def in_block_starts():
    st = []
    for h in range(H):
        st += [h * 256, 1024 + h * 256, 2048 + h * 512, 2048 + h * 512 + 256, 4096 + h * 512, 4096 + h * 512 + 256,
               7184 + h * 512, 7184 + h * 512 + 256, 9232 + h * 512, 9232 + h * 512 + 256, 6160 + h * 256]
    return st


def relayout_weights(w_in, w_out, w_up, w_down, cfg):
    def colblk(w, c0, width=WB):
        kc = w.shape[0] // P
        return w[:, c0:c0 + width].reshape(kc, P, width).transpose(1, 0, 2)
    out = {}
    out["w_in_blk"] = np.ascontiguousarray(np.stack([colblk(w_in, c0) for c0 in in_block_starts()]))
    out["w_alr"] = np.ascontiguousarray(colblk(w_in, 6144, 16))
    out["w_out_blk"] = np.ascontiguousarray(np.stack([colblk(w_out, c0) for c0 in range(0, D, WB)]))
    out["w_up_blk"] = np.ascontiguousarray(np.stack([colblk(w_up, c0) for c0 in range(0, cfg.ff, WB)]))
    dn = []
    for f_ in range(cfg.ff // cfg.fsb):
        rows = w_down[f_ * cfg.fsb:(f_ + 1) * cfg.fsb]
        for c0 in range(0, D, WB):
            dn.append(colblk(rows, c0))
    out["w_down_blk"] = np.ascontiguousarray(np.stack(dn))
    return out


_NC_CACHE = {}


def _get_nc(cfg_key, cfg):
    if cfg_key not in _NC_CACHE:
        _NC_CACHE[cfg_key] = build_program(cfg)
    return _NC_CACHE[cfg_key]


def kernel(x_prompt, x_sample, state_gla, state_pool, norm_mix_g, w_in, w_alpha_up, b_alpha, gla_norm_g,
           pool_w, pool_scale, w_out, norm_mlp_g, w_up, w_down, norm_final_g):
    f = lambda a: np.ascontiguousarray(np.asarray(a, dtype=np.float32))
    x_prompt, x_sample, state_gla, state_pool = f(x_prompt), f(x_sample), f(state_gla), f(state_pool)
    B, T, _ = x_prompt.shape
    NCORES = 8
    half = T // 2
    cfg = Cfg()
    nc = _get_nc("full", cfg)
    shared = {"norm_mix_g": f(norm_mix_g), "w_alpha_up": f(w_alpha_up), "b_alpha": f(b_alpha),
              "gla_norm_g": f(gla_norm_g), "pool_w": f(pool_w), "pool_scale": f(pool_scale),
              "norm_mlp_g": f(norm_mlp_g), "norm_final_g": f(norm_final_g)}
    shared.update(relayout_weights(f(w_in), f(w_out), f(w_up), f(w_down), cfg))
    consts = [make_consts(True), make_consts(False)]
    zeros_pre = np.zeros((half, D), np.float32)
    in_maps = []
    for c in range(NCORES):
        b, hf = c // 2, c % 2
        xs = x_sample[c * 16:(c + 1) * 16].reshape(16 * 8, D)
        xmain = np.concatenate([x_prompt[b, hf * half:(hf + 1) * half], xs], axis=0)
        xpre = zeros_pre if hf == 0 else x_prompt[b, 0:half]
        cm, selr = consts[hf]
        m = dict(shared)
        m.update({"xpre": np.ascontiguousarray(xpre), "xmain": np.ascontiguousarray(xmain),
                  "sgla": state_gla[c * 16:(c + 1) * 16], "spool": state_pool[c * 16:(c + 1) * 16],
                  "cmat": cm, "selr": selr})
        in_maps.append(m)
    res = run_bass_kernel_spmd(nc, in_maps, core_ids=list(range(NCORES)))
    R = res.results
    y_p = np.empty((B, T, D), np.float32)
    y_s = np.empty((128, 8, D), np.float32)
    sg_p = np.empty((B, H, DK, DV), np.float32)
    pl_p = np.empty((B, 15, 1024), np.float32)
    sg_s = np.empty((128, H, DK, DV), np.float32)
    pl_s = np.empty((128, 15, 1024), np.float32)
    for c in range(NCORES):
        b, hf = c // 2, c % 2
        yc = R[c]["y"]
        y_p[b, hf * half:(hf + 1) * half] = yc[0:half]
        y_s[c * 16:(c + 1) * 16] = yc[half:half + 128].reshape(16, 8, D)
        if hf == 1:
            sg_p[b] = R[c]["o_sgp"]
            pl_p[b] = R[c]["o_plp"]
        sg_s[c * 16:(c + 1) * 16] = R[c]["o_sgs"]
        pl_s[c * 16:(c + 1) * 16] = R[c]["o_pls"]
    return (y_p, y_s, sg_p, pl_p, sg_s, pl_s)
```
